# Optimizing a Trainium2 kernel written in Bass

```python
import functools
import jax, jax.numpy as jnp
from jax import lax
import numpy as np

D_MODEL = 1024
BATCH = 2
SEQ = 8192
DEPTH = 1
DEC_BATCH = 32
DEC_SEQ = 64
PAST_LEN = 2048

CHUNK = 64
LEFT_CHUNKS = 8
WINDOW_ROWS = LEFT_CHUNKS * CHUNK
BAND = (LEFT_CHUNKS + 1) * CHUNK
D_MIX = D_MODEL
W_A = D_MIX // 2
H_A = 8
DH_A = W_A // H_A
W_B = D_MIX - W_A
H_B = 4
DK_B = W_B // H_B
DV_B = W_B // H_B
REL_CLIP = 128
GLA_BLOCK = 16
EPS = 1e-6
ATTN_SCALE = DH_A ** -0.5
NEG_INF = -1e30
N_IN = 4 * W_A + 4 * W_B
SPLITS = [W_A, 2 * W_A, 3 * W_A, 4 * W_A, 4 * W_A + W_B, 4 * W_A + 2 * W_B, 4 * W_A + 3 * W_B]

kernel_name = 'hymba_chunkattn_hgrn2_stream_step'


def rms_norm(x, g):
    xf = x.astype(jnp.float32)
    y = xf * lax.rsqrt(jnp.mean(xf * xf, axis=-1, keepdims=True) + EPS)
    return (y * g.astype(jnp.float32)).astype(x.dtype)


def band_attention(q, k, v, rel, valid, rel_bias):
    idx = jnp.clip(rel, -REL_CLIP, REL_CLIP) + REL_CLIP
    bias = rel_bias[:, idx].astype(jnp.float32)
    s = jnp.einsum('bnqhd,bnkhd->bnhqk', q, k).astype(jnp.float32) * ATTN_SCALE + bias
    s = jnp.where(valid[None, :, None], s, NEG_INF)
    p = jax.nn.softmax(s, axis=-1).astype(v.dtype)
    return jnp.einsum('bnhqk,bnkhd->bnqhd', p, v)


def prompt_attention(q, k, v, rel_bias):
    B, T, H, Dh = q.shape
    nc = T // CHUNK
    pad = ((0, 0), (WINDOW_ROWS, 0), (0, 0), (0, 0))
    kc = jnp.pad(k, pad).reshape(B, nc + LEFT_CHUNKS, CHUNK, H, Dh)
    vc = jnp.pad(v, pad).reshape(B, nc + LEFT_CHUNKS, CHUNK, H, Dh)
    kb = jnp.concatenate([kc[:, j:j + nc] for j in range(LEFT_CHUNKS + 1)], axis=2)
    vb = jnp.concatenate([vc[:, j:j + nc] for j in range(LEFT_CHUNKS + 1)], axis=2)
    qc = q.reshape(B, nc, CHUNK, H, Dh)
    band = jnp.arange(BAND)
    rel = jnp.arange(CHUNK)[:, None] + WINDOW_ROWS - band[None, :]
    key_chunk = jnp.arange(nc)[:, None] - LEFT_CHUNKS + (band // CHUNK)[None, :]
    valid = (key_chunk >= 0)[:, None, :]
    return band_attention(qc, kb, vb, rel, valid, rel_bias).reshape(B, T, H, Dh)


def sample_attention(q, k, v, ck, cv, rel_bias):
    B, T, H, Dh = q.shape
    R = ck.shape[1]
    kk = jnp.concatenate([ck, k], axis=1)[:, None]
    vv = jnp.concatenate([cv, v], axis=1)[:, None]
    qpos = PAST_LEN + jnp.arange(T)
    kpos = jnp.concatenate([PAST_LEN - R + jnp.arange(R), qpos])
    rel = qpos[:, None] - kpos[None, :]
    qch = qpos // CHUNK
    kch = kpos // CHUNK
    valid = ((kch[None, :] >= qch[:, None] - LEFT_CHUNKS) & (kch[None, :] <= qch[:, None])
             & (kpos[None, :] >= 0))[None]
    return band_attention(q[:, None], kk, vv, rel, valid, rel_bias)[:, 0]


def gla_recurrence(q, k, v, logf, s0):
    B, T, H, DK = q.shape
    DV = v.shape[-1]
    f32 = jnp.float32
    q, k, v, logf = (a.astype(f32) for a in (q, k, v, logf))
    pad = (-T) % GLA_BLOCK
    if pad:
        pw = ((0, 0), (0, pad), (0, 0), (0, 0))
        q, k, v, logf = (jnp.pad(a, pw) for a in (q, k, v, logf))
    nb = (T + pad) // GLA_BLOCK

    def blocks(a):
        return a.reshape(B, nb, GLA_BLOCK, H, a.shape[-1]).swapaxes(0, 1)

    mask = jnp.tril(jnp.ones((GLA_BLOCK, GLA_BLOCK), dtype=bool))

    def step(S, inp):
        qb, kb, vb, gb = inp
        b = jnp.cumsum(gb, axis=1)
        qt = qb * jnp.exp(b)
        kt = kb * jnp.exp(-b)
        A = jnp.where(mask, jnp.einsum('bthk,bshk->bhts', qt, kt), 0.0)
        o = jnp.einsum('bthk,bhkv->bthv', qt, S) + jnp.einsum('bhts,bshv->bthv', A, vb)
        bl = b[:, -1]
        S = jnp.exp(bl)[..., None] * S + jnp.einsum('bshk,bshv->bhkv', kb * jnp.exp(bl[:, None] - b), vb)
        return S, o

    S, o = lax.scan(step, s0.astype(f32), (blocks(q), blocks(k), blocks(v), blocks(logf)))
    o = o.swapaxes(0, 1).reshape(B, nb * GLA_BLOCK, H, DV)[:, :T]
    return o, S.astype(s0.dtype)


def hgrn2_branch(hq, hf, hi, hg, lb, norm_g, s0):
    B, T, _ = hq.shape
    f = lb + (1.0 - lb) * jax.nn.sigmoid(hf.astype(jnp.float32))
    logf = jnp.log(f)
    kk = 1.0 - f
    q = jax.nn.silu(hq)
    heads = lambda a: a.reshape(B, T, H_B, a.shape[-1] // H_B)
    o, S = gla_recurrence(heads(q), heads(kk), heads(hi), heads(logf), s0)
    o = rms_norm(o.astype(hq.dtype), norm_g).reshape(B, T, W_B)
    return o * jax.nn.silu(hg), S


def mixer_layer(x, ln_g, w_in, lb, norm_g, w_out, attn_fn, s0):
    B, T, _ = x.shape
    aq, ak, av, ag, hq, hf, hi, hg = jnp.split(rms_norm(x, ln_g) @ w_in, SPLITS, axis=-1)
    heads = lambda a: a.reshape(B, T, H_A, DH_A)
    k = heads(ak)
    v = heads(av)
    o_a = attn_fn(heads(aq), k, v).reshape(B, T, W_A) * jax.nn.silu(ag)
    o_b, S = hgrn2_branch(hq, hf, hi, hg, lb, norm_g, s0)
    y = x + jnp.concatenate([o_a, o_b], axis=-1) @ w_out
    return y, k, v, S


def setup_inputs(seed: int = 0) -> dict:
    key = jax.random.key(seed)
    ks = jax.random.split(key, 12)
    f32 = jnp.float32
    R = min(WINDOW_ROWS, PAST_LEN)
    return {
        'x_prompt': jax.random.normal(ks[0], (BATCH, SEQ, D_MODEL), f32),
        'x_sample': jax.random.normal(ks[1], (DEC_BATCH, DEC_SEQ, D_MODEL), f32),
        'cache_attn_k': jax.random.normal(ks[2], (DEPTH, DEC_BATCH, R, H_A, DH_A), f32),
        'cache_attn_v': jax.random.normal(ks[3], (DEPTH, DEC_BATCH, R, H_A, DH_A), f32),
        'state_hgrn': 0.5 * jax.random.normal(ks[4], (DEPTH, DEC_BATCH, H_B, DK_B, DV_B), f32),
        'ln_in_g': 1.0 + 0.02 * jax.random.normal(ks[5], (DEPTH, D_MODEL), f32),
        'w_in': jax.random.normal(ks[6], (DEPTH, D_MODEL, N_IN), f32) * D_MODEL ** -0.5,
        'rel_bias': 0.1 * jax.random.normal(ks[7], (DEPTH, H_A, 2 * REL_CLIP + 1), f32),
        'lb_gamma': 0.1 * jax.random.normal(ks[8], (DEPTH + 1, W_B), f32),
        'hg_norm_g': 1.0 + 0.02 * jax.random.normal(ks[9], (DEPTH, H_B, DV_B), f32),
        'w_out': jax.random.normal(ks[10], (DEPTH, D_MIX, D_MODEL), f32) * D_MIX ** -0.5,
        'ln_f_g': 1.0 + 0.02 * jax.random.normal(ks[11], (D_MODEL,), f32),
    }


def reference(x_prompt, x_sample, cache_attn_k, cache_attn_v, state_hgrn, ln_in_g, w_in,
              rel_bias, lb_gamma, hg_norm_g, w_out, ln_f_g):
    lb_all = jnp.cumsum(jax.nn.softmax(lb_gamma.astype(jnp.float32), axis=0), axis=0)
    hp, hs = x_prompt, x_sample
    B, T, _ = x_prompt.shape
    prompt_rows = min(WINDOW_ROWS, T)
    s0_prompt = jnp.zeros((B, H_B, DK_B, DV_B), x_prompt.dtype)
    kp_l, vp_l, sp_l, ks_l, vs_l, ss_l = [], [], [], [], [], []
    for l in range(DEPTH):
        attn_p = functools.partial(prompt_attention, rel_bias=rel_bias[l])
        attn_s = functools.partial(sample_attention, ck=cache_attn_k[l], cv=cache_attn_v[l],
                                   rel_bias=rel_bias[l])
        hp, kp, vp, sp = mixer_layer(hp, ln_in_g[l], w_in[l], lb_all[l], hg_norm_g[l], w_out[l],
                                     attn_p, s0_prompt)
        hs, kd, vd, sd = mixer_layer(hs, ln_in_g[l], w_in[l], lb_all[l], hg_norm_g[l], w_out[l],
                                     attn_s, state_hgrn[l])
        kp_l.append(kp[:, T - prompt_rows:])
        vp_l.append(vp[:, T - prompt_rows:])
        sp_l.append(sp)
        ks_l.append(kd)
        vs_l.append(vd)
        ss_l.append(sd)
    y_prompt = rms_norm(hp, ln_f_g)
    y_sample = rms_norm(hs, ln_f_g)
    new_k_prompt = jnp.stack(kp_l)
    new_v_prompt = jnp.stack(vp_l)
    new_hgrn_prompt = jnp.stack(sp_l)
    new_k_sample = jnp.stack(ks_l)
    new_v_sample = jnp.stack(vs_l)
    new_hgrn_sample = jnp.stack(ss_l)
    return (y_prompt, y_sample, new_k_prompt, new_v_prompt, new_hgrn_prompt,
            new_k_sample, new_v_sample, new_hgrn_sample)
```

```python
import contextlib
import numpy as np
import concourse.bass as bass
import concourse.mybir as mybir
from concourse.bass_utils import run_bass_kernel_spmd

F32 = mybir.dt.float32
BF16 = mybir.dt.bfloat16
AF = mybir.ActivationFunctionType
ALU = mybir.AluOpType
AX = mybir.AxisListType

EPS = 1e-6
USE_CC = False
NPB = 4
STOP = 99
MAIN_BLOCKS = None
XOFF = 0
YOFF = 0
NOFF = 0
YQ = 'pool'
TESTY = -1


class Sched:
    def __init__(self, nc, es):
        self.nc = nc
        self.K = 4
        self.KD = 8
        self.ce = ('pe', 'act', 'dve', 'pool')
        self.streams = {e: [] for e in ('pe', 'act', 'dve', 'pool', 'sp')}
        self.nops = {e: 0 for e in self.ce}
        self.ndma = {'sp': 0, 'pool': 0}
        self.sems = {}
        for e in self.ce:
            for i in range(self.K):
                self.sems[('c', e, i)] = es.enter_context(nc.semaphore(f"c_{e}_{i}"))
        for q in ('sp', 'pool'):
            for i in range(self.KD):
                self.sems[('d', q, i)] = es.enter_context(nc.semaphore(f"d_{q}_{i}"))
        self.dval = {q: [0] * self.KD for q in ('sp', 'pool')}
        self.dlast = {q: [None] * self.KD for q in ('sp', 'pool')}
        self.res = {}
        self.waited = {e: {} for e in self.streams}
        self.eclock = {}
        self.clock = {}
        self.seq = {}
        self.nseq = 0
        self.nbank = 0
        self.pools = {'g': [0, 1, 2, 3, 4, 5], 'a': [0, 1, 2, 3]}
        self.pctr = {}

    def bank(self, pool='g'):
        lst = self.pools[pool]
        i = self.pctr.get(pool, 0)
        self.pctr[pool] = i + 1
        return lst[i % len(lst)]

    def add(self, eng, fn, R=(), W=(), dma=False, ninst=1):
        deps = []
        for k in R:
            r = self.res.get(k)
            if r is not None and r[0] is not None:
                deps.append(r[0])
        for k in W:
            r = self.res.get(k)
            if r is not None:
                if r[0] is not None:
                    deps.append(r[0])
                deps.extend(r[1])
        if dma:
            q = eng
            j = self.ndma[q]
            self.ndma[q] += 1
            slot = j % self.KD
            semid = ('d', q, slot)
            if self.dlast[q][slot] is not None:
                deps.append(self.dlast[q][slot])
            self.dval[q][slot] += 16 * ninst
            val = self.dval[q][slot]
        else:
            idx = self.nops[eng]
            self.nops[eng] += 1
            semid = ('c', eng, idx % self.K)
            val = idx // self.K + 1
        me = (semid, val, eng)
        if dma:
            self.dlast[eng][slot] = me
        K = self.eclock.setdefault(eng, {})

        def known(d):
            sid, v, e = d
            if sid[0] == 'c':
                return K.get(('c', e), -1) >= (v - 1) * self.K + sid[2]
            return K.get(sid, 0) >= v

        need = {}
        for d in sorted(set(deps), key=lambda d: -self.seq.get((d[0], d[1]), 0)):
            sid, v, e = d
            if e == 'pe' and eng == 'pe' and not dma:
                continue
            if known(d):
                continue
            if need.get(sid, 0) < v:
                need[sid] = v
            for kk_, vv_ in self.clock.get((sid, v), {}).items():
                if K.get(kk_, -1) < vv_:
                    K[kk_] = vv_
        myclock = dict(K)
        if dma:
            myclock[semid] = val
        else:
            myclock[('c', eng)] = idx
        self.clock[(semid, val)] = myclock
        self.nseq += 1
        self.seq[(semid, val)] = self.nseq
        self.streams[eng].append((list(need.items()), fn, semid, dma))
        for k in R:
            r = self.res.setdefault(k, [None, []])
            r[1].append(me)
        for k in W:
            self.res[k] = [me, []]
        return me

    def emit(self, block):
        nc = self.nc

        def run(name, eng):
            for waits, fn, semid, dma in self.streams[name]:
                fold = (name in ('pe', 'act', 'dve', 'pool')) and (not dma) and len(waits) > 0
                for sid, v in (waits[:-1] if fold else waits):
                    eng.wait_ge(self.sems[sid], v)
                insts = fn(eng)
                if fold:
                    insts[0]._wait_ge(self.sems[waits[-1][0]], waits[-1][1])
                if dma:
                    for i in insts:
                        i.then_inc(self.sems[semid], 16)
                else:
                    insts[-1].then_inc(self.sems[semid], 1)
            if name == 'pool':
                for q in ('sp', 'pool'):
                    for i in range(self.KD):
                        if self.dval[q][i] > 0:
                            eng.wait_ge(self.sems[('d', q, i)], self.dval[q][i])

        @block.tensor
        def _(e):
            run('pe', e)

        @block.scalar
        def _(e):
            run('act', e)

        @block.vector
        def _(e):
            run('dve', e)

        @block.gpsimd
        def _(e):
            run('pool', e)

        @block.sync
        def _(e):
            run('sp', e)


def build_program():
    nc = bass.Bass("TRN2", target_bir_lowering=False)

    def din(name, shape):
        return nc.dram_tensor(name, list(shape), F32, kind="ExternalInput")

    def dout(name, shape):
        return nc.dram_tensor(name, list(shape), F32, kind="ExternalOutput")

    d_xp = din("xp", [2048, 1024]); d_xh = din("xh", [512, 1024]); d_xs = din("xs", [256, 1024])
    d_xpre = din("xpre", [6144, 1024])
    d_ck = din("ck", [4, 512, 512]); d_cv = din("cv", [4, 512, 512]); d_st = din("st", [4, 4, 128, 128])
    d_win = din("w_in", [1024, 4096]); d_wout = din("w_out", [1024, 1024])
    d_gin = din("gin", [128, 8]); d_gf = din("gf", [1, 1024]); d_gn = din("gn", [1, 512])
    d_lbg = din("lbg", [2, 512]); d_tz3 = din("tz3", [128, 1024]); d_tz4 = din("tz4", [128, 1024]); d_rbfar = din("rbfar", [1, 8])
    d_cst = din("cst", [128, 512]); d_pc = din("pc", [128, 8])
    o_ypl = [dout(f"yp{i}", [512, 1024]) for i in range(4)]; o_ys = dout("ys", [256, 1024])
    o_kp = dout("kp", [512, 512]); o_vp = dout("vp", [512, 512]); o_sp = dout("spo", [4, 128, 128])
    o_ks = dout("ks", [256, 512]); o_vs = dout("vs", [256, 512]); o_ss = dout("sso", [4, 4, 128, 128])

    es = contextlib.ExitStack()
    with es:
        S = Sched(nc, es)

        def sb(name, shape, dt):
            return es.enter_context(nc.sbuf_tensor(name, list(shape), dt))

        ps = [es.enter_context(nc.psum_tensor(f"ps{i}", [128, 512], F32)) for i in range(8)]

        def psf(b):
            return ps[b][:, :]

        def psb(b):
            return ps[b][:, :].bitcast(BF16)

        win = sb("win", [128, 8, 4096], BF16)
        wout = sb("wout", [128, 8, 1024], BF16)
        xnT = sb("xnT", [128, 8, 512], BF16)
        xin = sb("xin", [128, 2, 1024], F32)
        xsb = sb("xsb", [128, 2, 1024], BF16)
        xres = sb("xres", [128, 2, 1024], F32)
        kTr = sb("kTr", [128, 2, 4, 512], BF16)
        vr = sb("vr", [128, 2, 4, 8, 65], BF16)
        qTz = sb("qTz", [128, 8, 512], BF16)
        hqs = sb("hqs", [128, 4, 512], BF16)
        tfm = sb("tfm", [128, 512], F32)
        bufA = sb("bufA", [128, 512], F32); bufB = sb("bufB", [128, 512], F32); bufC = sb("bufC", [128, 512], F32)
        tta = sb("tta", [128, 512], F32); ttg = tta; g2f = sb("g2f", [128, 512], F32)
        kt = sb("kt", [128, 512], BF16); vb = sb("vb", [128, 512], BF16)
        ebT = sb("ebT", [128, 4, 128], F32); qtT = sb("qtT", [128, 4, 128], BF16)
        ktT = sb("ktT", [128, 4, 128], BF16); ATm = sb("ATm", [128, 4, 128], BF16)
        gbg = sb("gbg", [128, 512], BF16); ga2 = sb("ga2", [128, 512], BF16)
        Sa = sb("Sa", [128, 4, 128], F32); Sb_ = sb("Sb", [128, 4, 128], F32); tmpS = sb("tmpS", [128, 4, 128], F32)
        Sbf0 = sb("Sbf0", [128, 4, 128], BF16); Sbf1 = sb("Sbf1", [128, 4, 128], BF16)
        ebp = sb("ebp", [128, 3, 8], F32)
        pT = sb("pT", [128, 2, 5, 128], BF16)
        oa = sb("oa", [128, 512], BF16); ob = sb("ob", [128, 512], BF16)
        oT = sb("oT", [128, 8, 128], BF16)
        osb = sb("osb", [128, 512], F32); osq = sb("osq", [128, 512], F32)
        yres = sb("yres", [128, 1024], F32); yout = sb("yout", [128, 1, 1024], F32)
        kvst = sb("kvst", [128, 2, 512], F32)
        small = sb("small", [128, 64], F32)
        cstf = sb("cstf", [128, 512], F32)
        ident = sb("ident", [128, 128], BF16)
        Dm0 = sb("Dm0", [128, 128], BF16); D3 = sb("D3", [128, 8, 128], BF16); D4 = sb("D4", [128, 8, 128], BF16)
        cfar = sb("cfar", [128, 8], F32); gcol = sb("gcol", [128, 8], F32); pcs = sb("pcs", [128, 8], F32)
        C0 = sb("C0", [128, 512], F32); C1 = sb("C1", [128, 512], F32)
        Gn = sb("Gn", [128, 512], F32); Gf = sb("Gf", [128, 1024], F32)
        Lsave = sb("Lsave", [128, 512], F32)

        tri = cstf[:, 128:256]
        def sm(a, b):
            return small[:, a:b]

        def op(eng, fn, R=(), W=()):
            return S.add(eng, fn, R, W)

        def dma(q, fn, R=(), W=(), n=1):
            return S.add(q, fn, R, W, dma=True, ninst=n)

        def act(out, in_, func, R, W, bias=0.0, scale=1.0, eng='act'):
            op('act', lambda e: [e.activation(out=out, in_=in_, func=func, bias=bias, scale=scale)], R, W)

        def tt(eng, out, in0, in1, o, R, W):
            op(eng, lambda e: [e.tensor_tensor(out=out, in0=in0, in1=in1, op=o)], R, W)

        def ts(eng, out, in0, s1, s2, o0, o1, R, W):
            if s2 is None:
                op(eng, lambda e: [e.tensor_scalar(out=out, in0=in0, scalar1=s1, scalar2=None, op0=o0)], R, W)
            else:
                op(eng, lambda e: [e.tensor_scalar(out=out, in0=in0, scalar1=s1, scalar2=s2, op0=o0, op1=o1)], R, W)

        def stt(out, in0, scalar, in1, o0, o1, R, W, accum=None):
            if accum is None:
                op('dve', lambda e: [e.scalar_tensor_tensor(out=out, in0=in0, scalar=scalar, in1=in1, op0=o0, op1=o1)], R, W)
            else:
                op('dve', lambda e: [e.scalar_tensor_tensor(out=out, in0=in0, scalar=scalar, in1=in1, op0=o0, op1=o1,
                                                            accum_out=accum)], R, W)

        def cp(eng, out, in_, R, W):
            if eng == 'act':
                op('act', lambda e: [e.copy(out=out, in_=in_)], R, W)
            else:
                op(eng, lambda e: [e.tensor_copy(out=out, in_=in_)], R, W)

        def rstd_from(ssum, lnv, rs, n, R, W, key):
            act(lnv, ssum, AF.Ln, R, [key], bias=EPS, scale=1.0 / n)
            act(rs, lnv, AF.Exp, [key], W, scale=-0.5)

        if TESTY >= 0:
            dma('sp', lambda e: [e.dma_start(out=o_ypl[TESTY // 512].ap()[TESTY % 512:TESTY % 512 + 128, :], in_=yout[:, 0, :])], R=[('yout', 0)], W=['oy'])
        dma('pool', lambda e: [e.dma_start(out=cstf[:, :], in_=d_cst.ap())], W=['cstf'])
        dma('pool', lambda e: [e.dma_start(out=gcol[:, :], in_=d_gin.ap())], W=['gcol'])
        dma('pool', lambda e: [e.dma_start(out=pcs[:, :], in_=d_pc.ap())], W=['pcs'])
        dma('pool', lambda e: [e.dma_start(out=cfar[:, :], in_=d_rbfar.ap().broadcast_to([128, 8]))], W=['cfar'])
        dma('pool', lambda e: [e.dma_start(out=Gn[:, :], in_=d_gn.ap().broadcast_to([128, 512]))], W=['Gn'])
        dma('pool', lambda e: [e.dma_start(out=Gf[:, :], in_=d_gf.ap().broadcast_to([128, 1024]))], W=['Gf'])
        dma('pool', lambda e: [e.dma_start(out=bufA[:, :], in_=d_lbg.ap()[0:1, :].broadcast_to([128, 512]))], W=['bufA'])
        dma('pool', lambda e: [e.dma_start(out=bufB[:, :], in_=d_lbg.ap()[1:2, :].broadcast_to([128, 512]))], W=['bufB'])
        tt('dve', bufC[:, :], bufA[:, :], bufB[:, :], ALU.subtract, ['bufA', 'bufB'], ['bufC'])
        act(bufA[:, :], bufC[:, :], AF.Tanh, ['bufC'], ['bufA'], scale=0.5)
        ts('dve', C0[:, :], bufA[:, :], 0.5, 0.5, ALU.mult, ALU.add, ['bufA'], ['C0'])
        ts('dve', C1[:, :], bufA[:, :], -0.5, 0.5, ALU.mult, ALU.add, ['bufA'], ['C1'])
        cp('dve', ident[:, :], cstf[:, 0:128], ['cstf'], ['ident'])
        cp('dve', Dm0[:, :], cstf[:, 256:384], ['cstf'], ['Dm0'])
        tzA = kTr[:, 0, :, :].rearrange("p a b -> p (a b)").bitcast(F32)
        tzB = kTr[:, 1, :, :].rearrange("p a b -> p (a b)").bitcast(F32)
        dma('pool', lambda e: [e.dma_start(out=tzA, in_=d_tz3.ap())], W=[('kT', 0)])
        dma('pool', lambda e: [e.dma_start(out=tzB, in_=d_tz4.ap())], W=[('kT', 1)])

        def build_bias_tiles():
            for h in range(8):
                ts('dve', D3[:, h, :], tzA[:, h * 128:(h + 1) * 128], cfar[:, h:h + 1], 8.0, ALU.subtract, ALU.mult,
                   [('kT', 0), 'cfar'], [('D3', h)])
                ts('dve', tzB[:, h * 128:(h + 1) * 128], tzB[:, h * 128:(h + 1) * 128], cfar[:, h:h + 1], 8.0,
                   ALU.subtract, ALU.mult, [('kT', 1), 'cfar'], [('kT', 1)])
                tt('dve', D4[:, h, :], tzB[:, h * 128:(h + 1) * 128], cstf[:, 384:512], ALU.add,
                   [('kT', 1), 'cstf'], [('D4', h)])
            op('pool', lambda e: [e.memset(qTz[:, :, :], 0.0)], W=['qTz'])
            op('pool', lambda e: [e.memset(vr[:, :, :, :, 64:65], 1.0)], W=[('vr', 0), ('vr', 1)])

        op('pool', lambda e: [e.memset(Sa[:, :, :], 0.0)], W=['Sa'])

        wslots = [xres[:, 0, 0:512], xres[:, 0, 512:1024], xres[:, 1, 0:512], xres[:, 1, 512:1024]]
        wkeys = [('xres', 0, 0), ('xres', 0, 1), ('xres', 1, 0), ('xres', 1, 1)]
        cast_engs = ['pool', 'dve', 'pool', 'act']
        wi = [0]

        def load_w(src_ap, dst_ap, scale_ap, wkey):
            i = wi[0]; wi[0] += 1
            slot = wslots[i % 4]; sk = wkeys[i % 4]
            dma('sp', lambda e: [e.dma_start(out=slot, in_=src_ap)], W=[sk])
            eng = cast_engs[i % 4]
            if eng == 'act':
                op('act', lambda e: [e.activation(out=dst_ap, in_=slot, func=AF.Copy, scale=scale_ap)], [sk, 'gcol'], [wkey])
            elif eng == 'pool':
                ts(eng, dst_ap, slot, scale_ap, 1.0, ALU.mult, ALU.mult, [sk, 'gcol'], [wkey])
            else:
                ts(eng, dst_ap, slot, scale_ap, None, ALU.mult, None, [sk, 'gcol'], [wkey])

        wtasks = []

        def load_win_cols(g):
            for kc in range(8):
                wtasks.append((d_win.ap()[kc * 128:(kc + 1) * 128, g * 512:(g + 1) * 512],
                               win[:, kc, g * 512:(g + 1) * 512], gcol[:, kc:kc + 1], ('win', g)))

        def load_wout():
            for kc in range(8):
                for n in range(2):
                    wtasks.append((d_wout.ap()[kc * 128:(kc + 1) * 128, n * 512:(n + 1) * 512],
                                   wout[:, kc, n * 512:(n + 1) * 512], 0.5, ('wout', n)))

        def pump(n):
            for _ in range(n):
                if wtasks:
                    load_w(*wtasks.pop(0))

        for g in (5, 6, 4, 7, 1, 2, 3, 0):
            load_win_cols(g)
        load_wout()

        nctr = [0]

        def stage_N1(xa, t, act_sq=False):
            s = nctr[0] % 2; nctr[0] += 1
            dma('sp', lambda e: [e.dma_start(out=xin[:, s, :], in_=xa)], W=[('xin', s)])
            if act_sq:
                op('act', lambda e: [e.activation(out=xsb[:, s, :], in_=xin[:, s, :], func=AF.Square, accum_out=sm(t, t + 1))],
                   [('xin', s)], [('xsb', s), ('ss', t)])
            else:
                stt(xsb[:, s, :], xin[:, s, :], 1.0, xin[:, s, :], ALU.mult, ALU.mult,
                    [('xin', s)], [('xsb', s), ('ss', t)], accum=sm(t, t + 1))
            rstd_from(sm(t, t + 1), sm(4 + t, 5 + t), sm(8 + t, 9 + t), 1024.0, [('ss', t)], [('rstd', t)], ('lnv', t))
            op('act', lambda e: [e.activation(out=xsb[:, s, :], in_=xin[:, s, :], func=AF.Copy, scale=sm(8 + t, 9 + t))],
               [('xin', s), ('rstd', t)], [('xsb', s)])
            return s

        def stage_N2(s, t):
            b = S.bank()
            op('pe', lambda e: [e.transpose(out=psb(b)[:, kc * 128:(kc + 1) * 128],
                                            in_=xsb[:, s, kc * 128:(kc + 1) * 128], identity=ident[:, :])
                                for kc in range(8)],
               [('xsb', s), 'ident'], [('ps', b)])
            cp('dve', xnT[:, :, t * 128:(t + 1) * 128],
               psb(b).rearrange("p (k t) -> p k t", k=8), [('ps', b)], [('xnT', t)])

        def stage_N_tile(xa, t, act_sq=False):
            stage_N2(stage_N1(xa, t, act_sq), t)

        def stage_N(x_aps):
            for t, xa in enumerate(x_aps):
                stage_N_tile(xa, t)
            return len(x_aps)

        def proj_tm(t, g):
            b = S.bank()
            op('pe', lambda e: [e.matmul(psf(b), lhsT=xnT[:, kc, t * 128:(t + 1) * 128],
                                         rhs=win[:, kc, g * 512:(g + 1) * 512], start=(kc == 0), stop=(kc == 7))
                                for kc in range(8)],
               [('xnT', t), ('win', g)], [('ps', b)])
            return b

        def proj_fm(c, nt):
            b = S.bank()
            op('pe', lambda e: [e.matmul(ps[b][:, 0:nt * 128], lhsT=win[:, kc, c * 128:(c + 1) * 128],
                                         rhs=xnT[:, kc, 0:nt * 128], start=(kc == 0), stop=(kc == 7))
                                for kc in range(8)],
               [('xnT', t) for t in range(nt)] + [('win', c // 4)], [('ps', b)])
            return b

        DEF = dict(A=(bufA, 'bufA'), B=(bufB, 'bufB'), C=(bufC, 'bufC'), kt=(kt[:, :], 'kt'), vb=(vb[:, :], 'vb'))

        def hgrn_gates(t, bs=DEF):
            (A, kA), (B, kB), (vbx, kv) = bs['A'], bs['B'], bs['vb']
            b = proj_tm(t, 5)
            act(A[:, :], psf(b), AF.Exp, [('ps', b)], [kA], scale=-1.0)
            act(A[:, :], A[:, :], AF.Ln, [kA], [kA], bias=1.0)
            act(A[:, :], A[:, :], AF.Exp, [kA], [kA], scale=-1.0)
            tt('pool', B[:, :], A[:, :], C1[:, :], ALU.mult, [kA, 'C1'], [kB])
            tt('dve', A[:, :], B[:, :], C0[:, :], ALU.add, [kB, 'C0'], [kA])
            ts('pool', B[:, :], A[:, :], -1.0, 1.0, ALU.mult, ALU.add, [kA], [kB])
            act(A[:, :], A[:, :], AF.Ln, [kA], [kA])
            b = proj_tm(t, 6)
            cp('act', vbx, psf(b), [('ps', b)], [kv])

        def hgrn_cum(t, full, bs=DEF, ebx=None, kex=None):
            (A, kA), (B, kB), (C, kC), (ktx, kk_) = bs['A'], bs['B'], bs['C'], bs['kt']
            b1 = S.bank()
            op('pe', lambda e: [e.matmul(psf(b1), lhsT=tri, rhs=A[:, :], start=True, stop=True)],
               [kA, 'cstf'], [('ps', b1)])
            act(C[:, :], psf(b1), AF.Exp, [('ps', b1)], [kC], scale=-1.0)
            tt('dve', ktx, B[:, :], C[:, :], ALU.mult, [kB, kC], [kk_])
            b2 = S.bank()
            if full:
                op('pe', lambda e: [e.matmul(ps[b2][:, h * 128:(h + 1) * 128], lhsT=A[:, h * 128:(h + 1) * 128],
                                             rhs=tri, start=True, stop=True) for h in range(4)],
                   [kA, 'cstf'], [('ps', b2)])
                act(ebT[:, :, :], psf(b2).rearrange("p (h t) -> p h t", h=4), AF.Exp, [('ps', b2)], ['ebT'])
                return (ebT[:, :, 63:64], ebT[:, :, 127:128])
            else:
                op('pe', lambda e: [e.matmul(ps[b2][:, h * 2:(h + 1) * 2], lhsT=A[:, h * 128:(h + 1) * 128],
                                             rhs=cstf[:, 191:256:64], start=True, stop=True) for h in range(4)],
                   [kA, 'cstf'], [('ps', b2)])
                ebv = ebx.rearrange("p (h t) -> p h t", h=4)
                act(ebv, ps[b2][:, 0:8].rearrange("p (h t) -> p h t", h=4), AF.Exp, [('ps', b2)], [kex])
                return (ebv[:, :, 0:1], ebv[:, :, 1:2])

        def hgrn_U(c, bs=DEF):
            (ktx, kk_), (vbx, kv) = bs['kt'], bs['vb']
            b = S.bank()
            op('pe', lambda e: [e.matmul(ps[b][:, h * 128:(h + 1) * 128], lhsT=ktx[c * 64:(c + 1) * 64, h * 128:(h + 1) * 128],
                                         rhs=vbx[c * 64:(c + 1) * 64, h * 128:(h + 1) * 128], start=True, stop=True)
                                for h in range(4)],
               [kk_, kv], [('ps', b)])
            return b

        def bc(ap):
            return ap.broadcast_to([128, 4, 128])

        def chain_step(Sin, kin, Sout, kout, bU, eb, keb='ebT'):
            tt('dve', tmpS[:, :, :], psf(bU).rearrange("p (h v) -> p h v", h=4), Sin[:, :, :], ALU.add,
               [('ps', bU), kin], ['tmpS'])
            tt('dve', Sout[:, :, :], tmpS[:, :, :], bc(eb), ALU.mult, ['tmpS', keb], [kout])

        PRE_T = 48 if STOP >= 3 else (8 if STOP == 2 else 0)
        osb_bf = osb[:, :].bitcast(BF16)
        osq_bf = osq[:, :].bitcast(BF16)
        setsA = [(bufA[:, :], 'bufA'), (tta[:, :], 'tta'), (yres[:, 0:512], ('yres', 0)), (yres[:, 512:1024], ('yres', 1))]
        setsB = [(bufB[:, :], 'bufB'), (g2f[:, :], 'g2f'), (kvst[:, 0, :], ('kvst', 0))]
        setsC = [(bufC[:, :], 'bufC'), (tfm[:, :], 'tfm')]
        setsK = [(kt[:, :], 'kt'), (osb_bf[:, 0:512], ('osbh', 0))]
        setsV = [(vb[:, :], 'vb'), (osb_bf[:, 512:1024], ('osbh', 1)), (osq_bf[:, 0:512], ('osqh', 0)),
                 (osq_bf[:, 512:1024], ('osqh', 1)), (ga2[:, :], 'ga2')]

        def pre_x(i):
            return d_xpre.ap()[i * 128:(i + 1) * 128, :]

        def pA1(i):
            t = i % 4
            A, kA = setsA[i % 4]
            V, kV = setsV[i % 5]
            b = proj_tm(t, 5)
            act(A, psf(b), AF.Exp, [('ps', b)], [kA], scale=-1.0)
            act(A, A, AF.Ln, [kA], [kA], bias=1.0)
            act(A, A, AF.Exp, [kA], [kA], scale=-1.0)
            b = proj_tm(t, 6)
            cp('dve', V, psf(b), [('ps', b)], [kV])

        def pA2(i):
            A, kA = setsA[i % 4]
            B, kB = setsB[i % 3]
            tt('pool', B, A, C1[:, :], ALU.mult, [kA, 'C1'], [kB])
            tt('dve', A, B, C0[:, :], ALU.add, [kB, 'C0'], [kA])
            ts('pool', B, A, -1.0, 1.0, ALU.mult, ALU.add, [kA], [kB])
            act(A, A, AF.Ln, [kA], [kA])

        def pB1(i):
            A, kA = setsA[i % 4]
            C, kC = setsC[i % 2]
            b1 = S.bank()
            op('pe', lambda e: [e.matmul(psf(b1), lhsT=tri, rhs=A, start=True, stop=True)], [kA, 'cstf'], [('ps', b1)])
            act(C, psf(b1), AF.Exp, [('ps', b1)], [kC], scale=-1.0)
            b2 = S.bank()
            op('pe', lambda e: [e.matmul(ps[b2][:, h * 2:(h + 1) * 2], lhsT=A[:, h * 128:(h + 1) * 128],
                                         rhs=cstf[:, 191:256:64], start=True, stop=True) for h in range(4)],
               [kA, 'cstf'], [('ps', b2)])
            ebv = ebp[:, i % 3, :].rearrange("p (h t) -> p h t", h=4)
            act(ebv, ps[b2][:, 0:8].rearrange("p (h t) -> p h t", h=4), AF.Exp, [('ps', b2)], [('ebp', i % 3)])

        def pB2(i):
            B, kB = setsB[i % 3]
            C, kC = setsC[i % 2]
            K, kK = setsK[i % 2]
            tt('pool', K, B, C, ALU.mult, [kB, kC], [kK])

        def pC(i):
            bs = dict(kt=setsK[i % 2], vb=setsV[i % 5])
            ebv = ebp[:, i % 3, :].rearrange("p (h t) -> p h t", h=4)
            bU0 = hgrn_U(0, bs)
            chain_step(Sa, 'Sa', Sb_, 'Sb', bU0, ebv[:, :, 0:1], ('ebp', i % 3))
            bU1 = hgrn_U(1, bs)
            chain_step(Sb_, 'Sb', Sa, 'Sa', bU1, ebv[:, :, 1:2], ('ebp', i % 3))

        if PRE_T >= 4:
            s0_ = stage_N1(pre_x(0), 0, act_sq=True)
            s1_ = stage_N1(pre_x(1), 1, act_sq=True)
            pump(8)
            stage_N2(s0_, 0)
            s2_ = stage_N1(pre_x(2), 2, act_sq=True)
            stage_N2(s1_, 1)
            s3_ = stage_N1(pre_x(3), 3, act_sq=True)
            pump(8)
            stage_N2(s2_, 2)
            stage_N2(s3_, 3)
        else:
            pump(16)
        nslot = {}
        spref = {}
        if PRE_T > 4:
            nslot[4] = stage_N1(pre_x(4), 0, act_sq=True)
        for i in range(PRE_T + 4 if PRE_T else 0):
            if i < PRE_T:
                pA1(i)
                if i + 4 < PRE_T:
                    stage_N2(nslot.pop(i + 4), i % 4)
                if i + 5 < PRE_T:
                    nslot[i + 5] = stage_N1(pre_x(i + 5), (i + 1) % 4, act_sq=True)
                pump(2)
            if 0 <= i - 1 < PRE_T:
                pA2(i - 1)
            if 0 <= i - 2 < PRE_T:
                pB1(i - 2)
            if 0 <= i - 3 < PRE_T:
                pB2(i - 3)
            if 0 <= i - 4 < PRE_T:
                pC(i - 4)
            if PRE_T and STOP >= 4 and i in (PRE_T - 4, PRE_T - 3):
                ts_ = i - (PRE_T - 4)
                spref[ts_] = stage_N1(d_xs.ap()[ts_ * 128:(ts_ + 1) * 128, :], ts_, act_sq=True)
        pump(1000)
        op('pool', lambda e: [e.memset(osb[:, 0:2], 0.0)], W=[('osbh', 0), ('osbh', 1), 'osb'])
        op('pool', lambda e: [e.memset(osq[:, 0:2], 0.0)], W=[('osqh', 0), ('osqh', 1), 'osq'])
        cp('pool', Lsave[:, :], Sa[:, :, :].rearrange("p h v -> p (h v)"), ['Sa'], ['Lsave'])

        def fold_start_state():
            cp('pool', Sa[:, :, :].rearrange("p h v -> p (h v)"), Lsave[:, :], ['Lsave'], ['Sa'])

        def fm_evacs(nt, slot, do_q=True):
            ntok = nt * 128
            for hp in range(4):
                b = proj_fm(4 + hp, nt)
                cp('dve' if hp % 2 else 'act', kTr[:, slot, hp, 0:ntok], ps[b][:, 0:ntok], [('ps', b)], [('kT', slot)])
                yield
            if not do_q:
                return
            for hp in range(4):
                b = proj_fm(hp, nt)
                cp('act', qTz[0:64, 2 * hp, 0:ntok], ps[b][0:64, 0:ntok], [('ps', b)], ['qTz'])
                cp('dve', qTz[64:128, 2 * hp + 1, 0:ntok], ps[b][64:128, 0:ntok], [('ps', b)], ['qTz'])
                yield
            for h in range(4):
                b = proj_fm(16 + h, nt)
                act(tfm[:, 0:ntok], ps[b][:, 0:ntok], AF.Tanh, [('ps', b)], ['tfm'], scale=0.5)
                stt(hqs[:, h, 0:ntok], tfm[:, 0:ntok], 1.0, ps[b][:, 0:ntok], ALU.add, ALU.mult, ['tfm', ('ps', b)], ['hqs'])
                yield

        def av_evac(t, slot, out_ap=None, vslot=0):
            b = proj_tm(t, 2)
            cp('act', vr[:, slot, t, :, 0:64], psf(b).rearrange("p (h d) -> p h d", h=8), [('ps', b)], [('vr', slot)])
            if out_ap is not None:
                cp('act', kvst[:, vslot, :], psf(b), [('ps', b)], [('kvst', vslot)])
                dma('pool', lambda e: [e.dma_start(out=out_ap, in_=kvst[:, vslot, :])], R=[('kvst', vslot)], W=['okv'])

        def ak_tm_out(t, out_ap, vslot=1):
            b = proj_tm(t, 1)
            cp('dve', kvst[:, vslot, :], psf(b), [('ps', b)], [('kvst', vslot)])
            dma('pool', lambda e: [e.dma_start(out=out_ap, in_=kvst[:, vslot, :])], R=[('kvst', vslot)], W=['okv2'])

        def gate_a(t):
            b = proj_tm(t, 3)
            act(tta[:, :], psf(b), AF.Tanh, [('ps', b)], ['tta'], scale=0.5)
            stt(ga2[:, :], tta[:, :], 1.0, psf(b), ALU.add, ALU.mult, ['tta', ('ps', b)], ['ga2'])

        def gate_b(t):
            b = proj_tm(t, 7)
            act(ttg[:, :], psf(b), AF.Tanh, [('ps', b)], ['tta'], scale=0.5)
            stt(g2f[:, :], ttg[:, :], 1.0, psf(b), ALU.add, ALU.mult, ['tta', ('ps', b)], ['g2f'])
            tt('pool', gbg[:, :], g2f[:, :], Gn[:, :], ALU.mult, ['g2f', 'Gn'], ['gbg'])

        def attn_evac(bO, half):
            r = sm(28 + 4 * half, 32 + 4 * half)
            pv = psf(bO)[:, 0:260].rearrange("p (h d) -> p h d", h=4)
            op('dve', lambda e: [e.reciprocal(out=r, in_=pv[:, :, 64:65].rearrange("p h o -> p (h o)"))],
               [('ps', bO)], [('rinv', half)])
            for hh in range(4):
                h = half * 4 + hh
                stt(oa[:, h * 64:(h + 1) * 64], pv[:, hh, 0:64], r[:, hh:hh + 1], ga2[:, h * 64:(h + 1) * 64],
                    ALU.mult, ALU.mult, [('ps', bO), ('rinv', half), 'ga2'], ['oa'])

        def attn_prompt(p, slot):
            pslot = 1 - slot
            S.pctr['a'] = 0
            srcs = []
            for j in range(5):
                rel = p - 4 + j
                srcs.append((pslot, 4 + rel) if rel < 0 else (slot, rel))

            def emit_qk(h):
                hp = h // 2
                pi = h % 2
                bX = S.bank('a'); bY = S.bank('a')

                def qk(e):
                    out = []
                    for j in range(5):
                        sl, tl = srcs[j]
                        dst = ps[bX][:, j * 128:(j + 1) * 128] if j < 4 else ps[bY][:, 0:128]
                        bias = {0: Dm0[:, :], 3: D3[:, h, :], 4: D4[:, h, :]}.get(j)
                        out.append(e.matmul(dst, lhsT=kTr[:, sl, hp, tl * 128:(tl + 1) * 128],
                                            rhs=qTz[:, h, p * 128:(p + 1) * 128], start=True, stop=(bias is None)))
                        if bias is not None:
                            out.append(e.matmul(dst, lhsT=ident[:, :], rhs=bias, start=False, stop=True))
                    return out
                op('pe', qk, [('kT', 0), ('kT', 1), 'qTz', 'ident', 'Dm0', ('D3', h), ('D4', h)], [('ps', bX), ('ps', bY)])
                act(pT[:, pi, 0:4, :], psf(bX).rearrange("p (j q) -> p j q", j=4), AF.Exp, [('ps', bX), 'cfar'],
                    [('pT', pi, 0)], bias=cfar[:, h:h + 1], scale=0.125)
                act(pT[:, pi, 4, :], ps[bY][:, 0:128], AF.Exp, [('ps', bY), 'cfar'], [('pT', pi, 1)],
                    bias=cfar[:, h:h + 1], scale=0.125)

            def emit_pv(h):
                pi = h % 2
                bO = 6
                hh = h % 4

                def pvf(e):
                    out = []
                    for j in range(5):
                        sl, tl = srcs[j]
                        out.append(e.matmul(ps[bO][:, hh * 65:(hh + 1) * 65], lhsT=pT[:, pi, j, :],
                                            rhs=vr[:, sl, tl, h, :], start=(j == 0), stop=(j == 4)))
                    return out
                op('pe', pvf, [('pT', pi, 0), ('pT', pi, 1), ('vr', 0), ('vr', 1)], [('ps', bO)])
                if hh == 3:
                    attn_evac(bO, h // 4)

            for h in range(9):
                if h < 8:
                    emit_qk(h)
                if h > 0:
                    emit_pv(h - 1)
                yield

        def hgrn_main(t, sample_states=None):
            e0, e1 = hgrn_cum(t, full=True)
            yield
            stt(qtT[:, :, :], hqs[:, :, t * 128:(t + 1) * 128], 0.5, ebT[:, :, :], ALU.mult, ALU.mult, ['hqs', 'ebT'], ['qtT'])
            b = S.bank()
            op('pe', lambda e: [e.transpose(out=psb(b)[:, h * 128:(h + 1) * 128], in_=kt[:, h * 128:(h + 1) * 128],
                                            identity=ident[:, :]) for h in range(4)], ['kt', 'ident'], [('ps', b)])
            cp('act', ktT[:, :, :], psb(b)[:, 0:512].rearrange("p (h t) -> p h t", h=4), [('ps', b)], ['ktT'])
            yield
            if sample_states is None:
                cp('pool', Sbf0[:, :, :], Sa[:, :, :], ['Sa'], ['Sbf0'])
                bU0 = hgrn_U(0)
                chain_step(Sa, 'Sa', Sb_, 'Sb', bU0, e0)
                yield
                cp('act', Sbf1[:, :, :], Sb_[:, :, :], ['Sb'], ['Sbf1'])
                bU1 = hgrn_U(1)
                chain_step(Sb_, 'Sb', Sa, 'Sa', bU1, e1)
                yield
            else:
                q0, q1 = sample_states
                dma('sp', lambda e: [e.dma_start(out=Sa[:, :, :], in_=d_st.ap()[q0].rearrange("h k v -> k h v")),
                                     e.dma_start(out=Sb_[:, :, :], in_=d_st.ap()[q1].rearrange("h k v -> k h v"))],
                    W=['Sa', 'Sb'], n=2)
                cp('pool', Sbf0[:, :, :], Sa[:, :, :], ['Sa'], ['Sbf0'])
                cp('pool', Sbf1[:, :, :], Sb_[:, :, :], ['Sb'], ['Sbf1'])
                for (Sx, kx, c, eb, q) in ((Sa, 'Sa', 0, e0, q0), (Sb_, 'Sb', 1, e1, q1)):
                    bU = hgrn_U(c)
                    tt('dve', tmpS[:, :, :], psf(bU).rearrange("p (h v) -> p h v", h=4), Sx[:, :, :], ALU.add,
                       [('ps', bU), kx], ['tmpS'])
                    tt('dve', osq[:, :].rearrange("p (h v) -> p h v", h=4), tmpS[:, :, :], bc(eb), ALU.mult,
                       ['tmpS', 'ebT'], ['osq'])
                    dma('pool', lambda e, q=q: [e.dma_start(out=o_ss.ap()[q].rearrange("h k v -> k h v"),
                                                            in_=osq[:, :].rearrange("p (h v) -> p h v", h=4))],
                        R=['osq'], W=['o_ss'])
                    yield
            bA = S.bank()
            op('pe', lambda e: [e.matmul(ps[bA][:, h * 128:(h + 1) * 128], lhsT=ktT[:, h, :], rhs=qtT[:, h, :],
                                         start=True, stop=True) for h in range(4)], ['ktT', 'qtT'], [('ps', bA)])
            tt('dve', ATm[:, :, :], psf(bA).rearrange("p (h t) -> p h t", h=4),
               tri.rearrange("p (o t) -> p o t", o=1).broadcast_to([128, 4, 128]), ALU.mult, [('ps', bA), 'cstf'], ['ATm'])
            yield
            bO = S.bank()

            def of(e):
                out = []
                for h in range(4):
                    cs = slice(h * 128, (h + 1) * 128)
                    out.append(e.matmul(ps[bO][0:64, cs], lhsT=qtT[:, h, 0:64], rhs=Sbf0[:, h, :], start=True, stop=False))
                    out.append(e.matmul(ps[bO][64:128, cs], lhsT=qtT[:, h, 64:128], rhs=Sbf1[:, h, :], start=True, stop=False))
                    out.append(e.matmul(ps[bO][:, cs], lhsT=ATm[:, h, :], rhs=vb[:, cs], start=False, stop=True))
                return out
            op('pe', of, ['qtT', 'Sbf0', 'Sbf1', 'ATm', 'vb'], [('ps', bO)])
            cp('act', osb[:, :], psf(bO), [('ps', bO)], ['osb'])
            tt('dve', osq[:, :], psf(bO), osb[:, :], ALU.mult, [('ps', bO), 'osb'], ['osq'])
            yield
            op('dve', lambda e: [e.tensor_reduce(out=sm(12, 16), in_=osq[:, :].rearrange("p (h v) -> p h v", h=4),
                                                 axis=AX.X, op=ALU.add)], ['osq'], ['ms_o'])
            rstd_from(sm(12, 16), sm(16, 20), sm(20, 24), 128.0, ['ms_o'], ['rstd_o'], 'ln_o')
            yield
            for h in range(4):
                cs = slice(h * 128, (h + 1) * 128)
                stt(ob[:, cs], osb[:, cs], sm(20 + h, 21 + h), gbg[:, cs], ALU.mult, ALU.mult, ['osb', 'rstd_o', 'gbg'], ['ob'])
            yield

        def wide(gen):
            first = True
            while True:
                narrow = S.pools['g']
                nctr_ = S.pctr.get('g', 0)
                S.pools['g'] = [0, 1, 2, 3, 4, 5]
                S.pctr['g'] = 0 if first else S.pctr.get('gw', 0)
                first = False
                try:
                    next(gen)
                except StopIteration:
                    S.pools['g'] = narrow
                    S.pctr['g'] = 0
                    return
                S.pctr['gw'] = S.pctr['g']
                S.pools['g'] = narrow
                S.pctr['g'] = nctr_
                yield

        def drive(*gens):
            gens = [g for g in gens if g is not None]
            while gens:
                for g in list(gens):
                    try:
                        next(g)
                    except StopIteration:
                        gens.remove(g)

        octr = [0]

        def out_stage(x_ap, y_ap):
            b = S.bank()
            op('pe', lambda e: [e.transpose(out=psb(b)[:, c * 128:(c + 1) * 128],
                                            in_=(oa if c < 4 else ob)[:, (c % 4) * 128:(c % 4 + 1) * 128],
                                            identity=ident[:, :]) for c in range(8)], ['oa', 'ob', 'ident'], [('ps', b)])
            cp('act', oT[:, :, :], psb(b).rearrange("p (k t) -> p k t", k=8), [('ps', b)], ['oT'])
            yield
            s = octr[0] % 2; octr[0] += 1
            ys_ = 0
            dma('sp', lambda e: [e.dma_start(out=xres[:, s, :], in_=x_ap)],
                W=[('xres', s, 0), ('xres', s, 1)])
            for n in range(2):
                bY = S.bank()
                op('pe', lambda e, n=n, bY=bY: [e.matmul(psf(bY), lhsT=oT[:, kc, :], rhs=wout[:, kc, n * 512:(n + 1) * 512],
                                                         start=(kc == 0), stop=(kc == 7)) for kc in range(8)],
                   ['oT', ('wout', n)], [('ps', bY)])
                tt('dve', yres[:, n * 512:(n + 1) * 512], psf(bY), xres[:, s, n * 512:(n + 1) * 512], ALU.add,
                   [('ps', bY), ('xres', s, n)], [('yres', n)])
                yield
            stt(yout[:, ys_, :], yres[:, :], 1.0, yres[:, :], ALU.mult, ALU.mult, [('yres', 0), ('yres', 1)],
                [('yout', ys_), 'ssy'], accum=sm(24, 25))
            yield
            rstd_from(sm(24, 25), sm(25, 26), sm(26, 27), 1024.0, ['ssy'], ['ry'], 'lny')
            yield
            stt(yout[:, ys_, :], yres[:, :], sm(26, 27), Gf[:, :], ALU.mult, ALU.mult, [('yres', 0), ('yres', 1), 'ry', 'Gf'],
                [('yout', ys_)])
            dma(YQ, lambda e: [e.dma_start(out=y_ap, in_=yout[:, ys_, :])], R=[('yout', ys_)], W=['oy'])

        def attn_sample(t):
            slot, pslot = 0, 1
            for e_ in range(2):
                q = 2 * t + e_
                for j in range(4):
                    s = nctr[0] % 2; nctr[0] += 1
                    dma('sp', lambda e, q=q, j=j, s=s: [e.dma_start(out=xin[:, s, 0:512], in_=d_ck.ap()[q, j * 128:(j + 1) * 128, :]),
                                                        e.dma_start(out=xin[:, s, 512:1024], in_=d_cv.ap()[q, j * 128:(j + 1) * 128, :])],
                        W=[('xin', s)], n=2)
                    cp('pool', xsb[:, s, 0:512], xin[:, s, 0:512], [('xin', s)], [('xsb', s)])
                    cp('act', vr[:, pslot, j, :, 0:64], xin[:, s, 512:1024].rearrange("p (h d) -> p h d", h=8),
                       [('xin', s)], [('vr', pslot)])
                    b = S.bank()
                    op('pe', lambda e, s=s, b=b: [e.transpose(out=psb(b)[:, hp * 128:(hp + 1) * 128],
                                                              in_=xsb[:, s, hp * 128:(hp + 1) * 128], identity=ident[:, :])
                                                  for hp in range(4)], [('xsb', s), 'ident'], [('ps', b)])
                    cp('dve', kTr[:, pslot, :, j * 128:(j + 1) * 128], psb(b)[:, 0:512].rearrange("p (h t) -> p h t", h=4),
                       [('ps', b)], [('kT', pslot)])
                    yield
                r0 = e_ * 64
                qc = slice(t * 128 + r0, t * 128 + r0 + 64)

                def emit_qk(h, r0=r0, qc=qc):
                    hp = h // 2
                    pi = h % 2
                    bX = S.bank('a'); bY = S.bank('a')

                    def qk(e):
                        out = []
                        for j in range(4):
                            dst = ps[bX][:, j * 128:j * 128 + 64]
                            bias = {0: Dm0[:, 0:64], 3: D3[:, h, 0:64]}.get(j)
                            out.append(e.matmul(dst, lhsT=kTr[:, pslot, hp, j * 128:(j + 1) * 128], rhs=qTz[:, h, qc],
                                                start=True, stop=(bias is None)))
                            if bias is not None:
                                out.append(e.matmul(dst, lhsT=ident[:, :], rhs=bias, start=False, stop=True))
                        dst = ps[bY][r0:r0 + 64, 0:64]
                        out.append(e.matmul(dst, lhsT=kTr[:, slot, hp, qc], rhs=qTz[:, h, qc], start=True, stop=False))
                        out.append(e.matmul(dst, lhsT=ident[0:64, 0:64], rhs=D4[0:64, h, 0:64], start=False, stop=True))
                        return out
                    op('pe', qk, [('kT', 0), ('kT', 1), 'qTz', 'ident', 'Dm0', ('D3', h), ('D4', h)], [('ps', bX), ('ps', bY)])
                    act(pT[:, pi, 0:4, 0:64], psf(bX).rearrange("p (j q) -> p j q", j=4)[:, :, 0:64], AF.Exp,
                        [('ps', bX), 'cfar'], [('pT', pi, 0)], bias=cfar[:, h:h + 1], scale=0.125)
                    act(pT[r0:r0 + 64, pi, 4, 0:64], ps[bY][r0:r0 + 64, 0:64], AF.Exp, [('ps', bY), 'cfar'],
                        [('pT', pi, 1)], bias=cfar[r0:r0 + 64, h:h + 1], scale=0.125)

                def emit_pv(h, r0=r0):
                    pi = h % 2
                    bO = 6 + h // 4
                    hh = h % 4

                    def pvf(e):
                        out = []
                        dst = ps[bO][r0:r0 + 64, hh * 65:(hh + 1) * 65]
                        for j in range(4):
                            out.append(e.matmul(dst, lhsT=pT[:, pi, j, 0:64], rhs=vr[:, pslot, j, h, :],
                                                start=(j == 0), stop=False))
                        out.append(e.matmul(dst, lhsT=pT[r0:r0 + 64, pi, 4, 0:64], rhs=vr[r0:r0 + 64, slot, t, h, :],
                                            start=False, stop=True))
                        return out
                    op('pe', pvf, [('pT', pi, 0), ('pT', pi, 1), ('vr', 0), ('vr', 1)], [('ps', bO)])

                for h in range(9):
                    if h < 8:
                        emit_qk(h)
                    if h > 0:
                        emit_pv(h - 1)
                    yield
            attn_evac(6, 0)
            attn_evac(7, 1)
            yield

        def sample_block():
            slot = 0
            for t in range(2):
                if t in spref:
                    stage_N2(spref.pop(t), t)
                else:
                    stage_N_tile(d_xs.ap()[t * 128:(t + 1) * 128, :], t)
            drive(fm_evacs(2, slot))
            for t in range(2):
                av_evac(t, slot, o_vs.ap()[t * 128:(t + 1) * 128, :], vslot=0)
                ak_tm_out(t, o_ks.ap()[t * 128:(t + 1) * 128, :], vslot=1)
            pend = None
            for t in range(2):
                hgrn_gates(t)
                gate_a(t)
                gate_b(t)
                drive(hgrn_main(t, sample_states=(2 * t, 2 * t + 1)), attn_sample(t), pend)
                pend = out_stage(d_xs.ap()[t * 128:(t + 1) * 128, :], o_ys.ap()[t * 128:(t + 1) * 128, :])
            return pend

        build_bias_tiles()
        S.pools['g'] = [4, 5]
        pend_s = sample_block() if STOP >= 4 else None

        NPB_main = 0 if STOP < 5 else (STOP - 4 if STOP < 8 else NPB)

        def halo_gen():
            hp_ = {}
            for t in range(4):
                if t not in hp_:
                    hp_[t] = stage_N1(d_xh.ap()[t * 128:(t + 1) * 128, :], t)
                if t + 1 < 4:
                    hp_[t + 1] = stage_N1(d_xh.ap()[(t + 1) * 128:(t + 2) * 128, :], t + 1)
                stage_N2(hp_.pop(t), t)
                yield
            yield from fm_evacs(4, 0, do_q=False)
            for t in range(4):
                av_evac(t, 0)
                yield
            cp('pool', vr[:, 0, :, :, 64:65].rearrange("p t h o -> p (t h o)"), pcs[:, 0:1].broadcast_to([128, 32]),
               ['pcs', ('vr', 0)], [('vr', 0)])

        drive(wide(halo_gen()), wide(pend_s) if pend_s is not None else None)

        fold_start_state()

        S.pools['g'] = [5, 7, 4]
        S.pctr['g'] = 0
        pending = [None]

        pref = {}

        def xtile(n, t):
            return d_xp.ap()[(n * 4 + t) * 128 + NOFF:(n * 4 + t + 1) * 128 + NOFF, :]

        def prefetch_gen(n):
            for t in range(2):
                pref[(n, t)] = stage_N1(xtile(n, t), t)
                yield

        def head_gen(n, slot):
            for t in range(4):
                if (n, t) not in pref:
                    pref[(n, t)] = stage_N1(xtile(n, t), t)
                if t + 1 < 4 and (n, t + 1) not in pref:
                    pref[(n, t + 1)] = stage_N1(xtile(n, t + 1), t + 1)
                stage_N2(pref.pop((n, t)), t)
                yield
            yield from fm_evacs(4, slot)

        def inproj_gen(t, slot, last):
            hgrn_gates(t)
            yield
            av_evac(t, slot, o_vp.ap()[t * 128:(t + 1) * 128, :] if last else None, vslot=0)
            yield
            if last:
                ak_tm_out(t, o_kp.ap()[t * 128:(t + 1) * 128, :], vslot=1)
                yield
            gate_a(t)
            yield
            gate_b(t)
            yield

        for n in (MAIN_BLOCKS if MAIN_BLOCKS is not None else range(NPB_main)):
            slot = (n + 1) % 2
            if n == 1:
                op('pool', lambda e: [e.memset(vr[:, 0, :, :, 64:65], 1.0)], R=[('vr', 0)], W=[('vr', 0)])
            drive(wide(head_gen(n, slot)), wide(pending[0]) if pending[0] is not None else None)
            pending[0] = None
            last = (n == NPB - 1)
            for t in range(4):
                row = (n * 4 + t) * 128
                drive(wide(inproj_gen(t, slot, last)))
                blocks_ = list(MAIN_BLOCKS if MAIN_BLOCKS is not None else range(NPB_main))
                nxt = blocks_[blocks_.index(n) + 1] if (t == 3 and blocks_.index(n) + 1 < len(blocks_)) else None
                drive(hgrn_main(t), pending[0], attn_prompt(t, slot), prefetch_gen(nxt) if nxt is not None else None)
                pending[0] = None
                pending[0] = out_stage(d_xp.ap()[row + XOFF:row + XOFF + 128, :],
                                       o_ypl[(row + YOFF) // 512].ap()[(row + YOFF) % 512:(row + YOFF) % 512 + 128, :])
        drive(pending[0])

        dma('pool', lambda e: [e.dma_start(out=o_sp.ap().rearrange("h k v -> k h v"), in_=Sa[:, :, :])],
            R=['Sa'], W=['o_sp'])

        with nc.Block() as block:
            S.emit(block)
    return nc


_NC_CACHE = {}


def _consts():
    cst = np.zeros((128, 512), np.float32)
    cst[:, 0:128] = np.eye(128, dtype=np.float32)
    s = np.arange(128)[:, None]; t = np.arange(128)[None, :]
    cst[:, 128:256] = ((s // 64 == t // 64) & (s <= t)).astype(np.float32)
    NEG = -30000.0
    k = s; q = t
    cst[:, 256:384] = np.where((q >= 64) & (k < 64), NEG, 0.0)
    cst[:, 384:512] = np.where((k >= 64) & (q < 64), NEG, 0.0)
    return cst


def kernel(x_prompt, x_sample, cache_attn_k, cache_attn_v, state_hgrn, ln_in_g, w_in,
           rel_bias, lb_gamma, hg_norm_g, w_out, ln_f_g):
    f = lambda a: np.ascontiguousarray(np.asarray(a, dtype=np.float32))
    x_prompt, x_sample = f(x_prompt), f(x_sample)
    ck, cv, st = f(cache_attn_k), f(cache_attn_v), f(state_hgrn)
    w_in_, w_out_ = f(w_in)[0], f(w_out)[0]
    rb = f(rel_bias)[0]
    kk_, qq_ = np.arange(128)[:, None], np.arange(128)[None, :]
    idx3 = np.clip(128 + qq_ - kk_, -128, 128) + 128
    idx4 = np.clip(qq_ - kk_, -128, 128) + 128
    tz3 = np.transpose(rb[:, idx3], (1, 0, 2)).reshape(128, 1024)
    tz4 = np.transpose(rb[:, idx4], (1, 0, 2)).reshape(128, 1024)
    common = {
        "w_in": w_in_, "w_out": w_out_,
        "gin": f(f(ln_in_g)[0].reshape(8, 128).T), "gf": f(ln_f_g).reshape(1, 1024),
        "gn": f(hg_norm_g).reshape(1, 512), "lbg": f(lb_gamma), "tz3": f(tz3), "tz4": f(tz4),
        "rbfar": f(rb[:, 256]).reshape(1, 8), "cst": _consts(),
    }
    in_maps = []
    for c in range(8):
        b, j = c // 4, c % 4
        pc = np.zeros((128, 8), np.float32)
        pc[:, 0] = 1.0 if j > 0 else 0.0
        for jp in range(4):
            pc[:, 1 + jp] = 1.0 if jp < j else 0.0
        m = dict(common)
        m["xp"] = x_prompt[b, 2048 * j:2048 * (j + 1)]
        m["xh"] = x_prompt[b, 2048 * j - 512:2048 * j] if j > 0 else np.zeros((512, 1024), np.float32)
        m["xs"] = x_sample[4 * c:4 * c + 4].reshape(256, 1024)
        xpre = np.zeros((6144, 1024), np.float32)
        if j > 0:
            xpre[6144 - 2048 * j:] = x_prompt[b, 0:2048 * j]
        m["xpre"] = xpre
        m["ck"] = ck[0, 4 * c:4 * c + 4].reshape(4, 512, 512)
        m["cv"] = cv[0, 4 * c:4 * c + 4].reshape(4, 512, 512)
        m["st"] = st[0, 4 * c:4 * c + 4]
        m["pc"] = pc
        in_maps.append({k: np.ascontiguousarray(v) for k, v in m.items()})
    if "nc" not in _NC_CACHE:
        _NC_CACHE["nc"] = build_program()
    nc = _NC_CACHE["nc"]
    res = run_bass_kernel_spmd(nc, in_maps, core_ids=list(range(8))).results
    y_prompt = np.zeros((2, 8192, 1024), np.float32)
    y_sample = np.zeros((32, 64, 1024), np.float32)
    nkp = np.zeros((1, 2, 512, 8, 64), np.float32); nvp = np.zeros((1, 2, 512, 8, 64), np.float32)
    nhp = np.zeros((1, 2, 4, 128, 128), np.float32)
    nks = np.zeros((1, 32, 64, 8, 64), np.float32); nvs = np.zeros((1, 32, 64, 8, 64), np.float32)
    nhs = np.zeros((1, 32, 4, 128, 128), np.float32)
    for c in range(8):
        b, j = c // 4, c % 4
        r = res[c]
        for i in range(4):
            y_prompt[b, 2048 * j + 512 * i:2048 * j + 512 * (i + 1)] = r[f"yp{i}"]
        y_sample[4 * c:4 * c + 4] = r["ys"].reshape(4, 64, 1024)
        nks[0, 4 * c:4 * c + 4] = r["ks"].reshape(4, 64, 8, 64)
        nvs[0, 4 * c:4 * c + 4] = r["vs"].reshape(4, 64, 8, 64)
        nhs[0, 4 * c:4 * c + 4] = r["sso"]
        if j == 3:
            nkp[0, b] = r["kp"].reshape(512, 8, 64)
            nvp[0, b] = r["vp"].reshape(512, 8, 64)
            nhp[0, b] = r["spo"]
    return (y_prompt, y_sample, nkp, nvp, nhp, nks, nvs, nhs)
```

```python
import contextlib
import numpy as np
import concourse.bass as bass
import concourse.mybir as mybir
from concourse.bass_utils import run_bass_kernel_spmd

F32 = mybir.dt.float32
BF16 = mybir.dt.bfloat16
AF = mybir.ActivationFunctionType
ALU = mybir.AluOpType
AX = mybir.AxisListType

EPS = 1e-6
USE_CC = False
NPB = 4
STOP = 99
MAIN_BLOCKS = None
XOFF = 0
YOFF = 0
NOFF = 0
YQ = 'pool'
TESTY = -1


class Sched:
    def __init__(self, nc, es):
        self.nc = nc
        self.K = 4
        self.KD = 8
        self.ce = ('pe', 'act', 'dve', 'pool')
        self.streams = {e: [] for e in ('pe', 'act', 'dve', 'pool', 'sp')}
        self.nops = {e: 0 for e in self.ce}
        self.ndma = {'sp': 0, 'pool': 0}
        self.sems = {}
        for e in self.ce:
            for i in range(self.K):
                self.sems[('c', e, i)] = es.enter_context(nc.semaphore(f"c_{e}_{i}"))
        for q in ('sp', 'pool'):
            for i in range(self.KD):
                self.sems[('d', q, i)] = es.enter_context(nc.semaphore(f"d_{q}_{i}"))
        self.dval = {q: [0] * self.KD for q in ('sp', 'pool')}
        self.dlast = {q: [None] * self.KD for q in ('sp', 'pool')}
        self.res = {}
        self.waited = {e: {} for e in self.streams}
        self.eclock = {}
        self.clock = {}
        self.seq = {}
        self.nseq = 0
        self.nbank = 0
        self.pools = {'g': [0, 1, 2, 3, 4, 5], 'a': [0, 1, 2, 3]}
        self.pctr = {}

    def bank(self, pool='g'):
        lst = self.pools[pool]
        i = self.pctr.get(pool, 0)
        self.pctr[pool] = i + 1
        return lst[i % len(lst)]

    def add(self, eng, fn, R=(), W=(), dma=False, ninst=1):
        deps = []
        for k in R:
            r = self.res.get(k)
            if r is not None and r[0] is not None:
                deps.append(r[0])
        for k in W:
            r = self.res.get(k)
            if r is not None:
                if r[0] is not None:
                    deps.append(r[0])
                deps.extend(r[1])
        if dma:
            q = eng
            j = self.ndma[q]
            self.ndma[q] += 1
            slot = j % self.KD
            semid = ('d', q, slot)
            if self.dlast[q][slot] is not None:
                deps.append(self.dlast[q][slot])
            self.dval[q][slot] += 16 * ninst
            val = self.dval[q][slot]
        else:
            idx = self.nops[eng]
            self.nops[eng] += 1
            semid = ('c', eng, idx % self.K)
            val = idx // self.K + 1
        me = (semid, val, eng)
        if dma:
            self.dlast[eng][slot] = me
        K = self.eclock.setdefault(eng, {})

        def known(d):
            sid, v, e = d
            if sid[0] == 'c':
                return K.get(('c', e), -1) >= (v - 1) * self.K + sid[2]
            return K.get(sid, 0) >= v

        need = {}
        for d in sorted(set(deps), key=lambda d: -self.seq.get((d[0], d[1]), 0)):
            sid, v, e = d
            if e == 'pe' and eng == 'pe' and not dma:
                continue
            if known(d):
                continue
            if need.get(sid, 0) < v:
                need[sid] = v
            for kk_, vv_ in self.clock.get((sid, v), {}).items():
                if K.get(kk_, -1) < vv_:
                    K[kk_] = vv_
        myclock = dict(K)
        if dma:
            myclock[semid] = val
        else:
            myclock[('c', eng)] = idx
        self.clock[(semid, val)] = myclock
        self.nseq += 1
        self.seq[(semid, val)] = self.nseq
        self.streams[eng].append((list(need.items()), fn, semid, dma))
        for k in R:
            r = self.res.setdefault(k, [None, []])
            r[1].append(me)
        for k in W:
            self.res[k] = [me, []]
        return me

    def emit(self, block):
        nc = self.nc

        def run(name, eng):
            for waits, fn, semid, dma in self.streams[name]:
                fold = (name in ('pe', 'act', 'dve', 'pool')) and (not dma) and len(waits) > 0
                for sid, v in (waits[:-1] if fold else waits):
                    eng.wait_ge(self.sems[sid], v)
                insts = fn(eng)
                if fold:
                    insts[0]._wait_ge(self.sems[waits[-1][0]], waits[-1][1])
                if dma:
                    for i in insts:
                        i.then_inc(self.sems[semid], 16)
                else:
                    insts[-1].then_inc(self.sems[semid], 1)
            if name == 'pool':
                for q in ('sp', 'pool'):
                    for i in range(self.KD):
                        if self.dval[q][i] > 0:
                            eng.wait_ge(self.sems[('d', q, i)], self.dval[q][i])

        @block.tensor
        def _(e):
            run('pe', e)

        @block.scalar
        def _(e):
            run('act', e)

        @block.vector
        def _(e):
            run('dve', e)

        @block.gpsimd
        def _(e):
            run('pool', e)

        @block.sync
        def _(e):
            run('sp', e)


def build_program():
    nc = bass.Bass("TRN2", target_bir_lowering=False)

    def din(name, shape):
        return nc.dram_tensor(name, list(shape), F32, kind="ExternalInput")

    def dout(name, shape):
        return nc.dram_tensor(name, list(shape), F32, kind="ExternalOutput")

    d_xp = din("xp", [2048, 1024]); d_xh = din("xh", [512, 1024]); d_xs = din("xs", [256, 1024])
    d_xpre = din("xpre", [6144, 1024])
    d_ck = din("ck", [4, 512, 512]); d_cv = din("cv", [4, 512, 512]); d_st = din("st", [4, 4, 128, 128])
    d_win = din("w_in", [1024, 4096]); d_wout = din("w_out", [1024, 1024])
    d_gin = din("gin", [128, 8]); d_gf = din("gf", [1, 1024]); d_gn = din("gn", [1, 512])
    d_lbg = din("lbg", [2, 512]); d_tz3 = din("tz3", [128, 1024]); d_tz4 = din("tz4", [128, 1024]); d_rbfar = din("rbfar", [1, 8])
    d_cst = din("cst", [128, 512]); d_pc = din("pc", [128, 8])
    o_ypl = [dout(f"yp{i}", [512, 1024]) for i in range(4)]; o_ys = dout("ys", [256, 1024])
    o_kp = dout("kp", [512, 512]); o_vp = dout("vp", [512, 512]); o_sp = dout("spo", [4, 128, 128])
    o_ks = dout("ks", [256, 512]); o_vs = dout("vs", [256, 512]); o_ss = dout("sso", [4, 4, 128, 128])

    es = contextlib.ExitStack()
    with es:
        S = Sched(nc, es)

        def sb(name, shape, dt):
            return es.enter_context(nc.sbuf_tensor(name, list(shape), dt))

        ps = [es.enter_context(nc.psum_tensor(f"ps{i}", [128, 512], F32)) for i in range(8)]

        def psf(b):
            return ps[b][:, :]

        def psb(b):
            return ps[b][:, :].bitcast(BF16)

        win = sb("win", [128, 8, 4096], BF16)
        wout = sb("wout", [128, 8, 1024], BF16)
        xnT = sb("xnT", [128, 8, 512], BF16)
        xin = sb("xin", [128, 2, 1024], F32)
        xsb = sb("xsb", [128, 2, 1024], BF16)
        xres = sb("xres", [128, 2, 1024], F32)
        kTr = sb("kTr", [128, 2, 4, 512], BF16)
        vr = sb("vr", [128, 2, 4, 8, 65], BF16)
        qTz = sb("qTz", [128, 8, 512], BF16)
        hqs = sb("hqs", [128, 4, 512], BF16)
        tfm = sb("tfm", [128, 512], F32)
        bufA = sb("bufA", [128, 512], F32); bufB = sb("bufB", [128, 512], F32); bufC = sb("bufC", [128, 512], F32)
        tta = sb("tta", [128, 512], F32); ttg = tta; g2f = sb("g2f", [128, 512], F32)
        kt = sb("kt", [128, 512], BF16); vb = sb("vb", [128, 512], BF16)
        ebT = sb("ebT", [128, 4, 128], F32); qtT = sb("qtT", [128, 4, 128], BF16)
        ktT = sb("ktT", [128, 4, 128], BF16); ATm = sb("ATm", [128, 4, 128], BF16)
        gbg = sb("gbg", [128, 512], BF16); ga2 = sb("ga2", [128, 512], BF16)
        Sa = sb("Sa", [128, 4, 128], F32); Sb_ = sb("Sb", [128, 4, 128], F32); tmpS = sb("tmpS", [128, 4, 128], F32)
        Sbf0 = sb("Sbf0", [128, 4, 128], BF16); Sbf1 = sb("Sbf1", [128, 4, 128], BF16)
        ebp = sb("ebp", [128, 3, 8], F32)
        pT = sb("pT", [128, 2, 5, 128], BF16)
        oa = sb("oa", [128, 512], BF16); ob = sb("ob", [128, 512], BF16)
        oT = sb("oT", [128, 8, 128], BF16)
        osb = sb("osb", [128, 512], F32); osq = sb("osq", [128, 512], F32)
        yres = sb("yres", [128, 1024], F32); yout = sb("yout", [128, 1, 1024], F32)
        kvst = sb("kvst", [128, 2, 512], F32)
        small = sb("small", [128, 64], F32)
        cstf = sb("cstf", [128, 512], F32)
        ident = sb("ident", [128, 128], BF16)
        Dm0 = sb("Dm0", [128, 128], BF16); D3 = sb("D3", [128, 8, 128], BF16); D4 = sb("D4", [128, 8, 128], BF16)
        cfar = sb("cfar", [128, 8], F32); gcol = sb("gcol", [128, 8], F32); pcs = sb("pcs", [128, 8], F32)
        C0 = sb("C0", [128, 512], F32); C1 = sb("C1", [128, 512], F32)
        Gn = sb("Gn", [128, 512], F32); Gf = sb("Gf", [128, 1024], F32)
        Lsave = sb("Lsave", [128, 512], F32)

        tri = cstf[:, 128:256]
        def sm(a, b):
            return small[:, a:b]

        def op(eng, fn, R=(), W=()):
            return S.add(eng, fn, R, W)

        def dma(q, fn, R=(), W=(), n=1):
            return S.add(q, fn, R, W, dma=True, ninst=n)

        def act(out, in_, func, R, W, bias=0.0, scale=1.0, eng='act'):
            op('act', lambda e: [e.activation(out=out, in_=in_, func=func, bias=bias, scale=scale)], R, W)

        def tt(eng, out, in0, in1, o, R, W):
            op(eng, lambda e: [e.tensor_tensor(out=out, in0=in0, in1=in1, op=o)], R, W)

        def ts(eng, out, in0, s1, s2, o0, o1, R, W):
            if s2 is None:
                op(eng, lambda e: [e.tensor_scalar(out=out, in0=in0, scalar1=s1, scalar2=None, op0=o0)], R, W)
            else:
                op(eng, lambda e: [e.tensor_scalar(out=out, in0=in0, scalar1=s1, scalar2=s2, op0=o0, op1=o1)], R, W)

        def stt(out, in0, scalar, in1, o0, o1, R, W, accum=None):
            if accum is None:
                op('dve', lambda e: [e.scalar_tensor_tensor(out=out, in0=in0, scalar=scalar, in1=in1, op0=o0, op1=o1)], R, W)
            else:
                op('dve', lambda e: [e.scalar_tensor_tensor(out=out, in0=in0, scalar=scalar, in1=in1, op0=o0, op1=o1,
                                                            accum_out=accum)], R, W)

        def cp(eng, out, in_, R, W):
            if eng == 'act':
                op('act', lambda e: [e.copy(out=out, in_=in_)], R, W)
            else:
                op(eng, lambda e: [e.tensor_copy(out=out, in_=in_)], R, W)

        def rstd_from(ssum, lnv, rs, n, R, W, key):
            act(lnv, ssum, AF.Ln, R, [key], bias=EPS, scale=1.0 / n)
            act(rs, lnv, AF.Exp, [key], W, scale=-0.5)

        if TESTY >= 0:
            dma('sp', lambda e: [e.dma_start(out=o_ypl[TESTY // 512].ap()[TESTY % 512:TESTY % 512 + 128, :], in_=yout[:, 0, :])], R=[('yout', 0)], W=['oy'])
        dma('pool', lambda e: [e.dma_start(out=cstf[:, :], in_=d_cst.ap())], W=['cstf'])
        dma('pool', lambda e: [e.dma_start(out=gcol[:, :], in_=d_gin.ap())], W=['gcol'])
        dma('pool', lambda e: [e.dma_start(out=pcs[:, :], in_=d_pc.ap())], W=['pcs'])
        dma('pool', lambda e: [e.dma_start(out=cfar[:, :], in_=d_rbfar.ap().broadcast_to([128, 8]))], W=['cfar'])
        dma('pool', lambda e: [e.dma_start(out=Gn[:, :], in_=d_gn.ap().broadcast_to([128, 512]))], W=['Gn'])
        dma('pool', lambda e: [e.dma_start(out=Gf[:, :], in_=d_gf.ap().broadcast_to([128, 1024]))], W=['Gf'])
        dma('pool', lambda e: [e.dma_start(out=bufA[:, :], in_=d_lbg.ap()[0:1, :].broadcast_to([128, 512]))], W=['bufA'])
        dma('pool', lambda e: [e.dma_start(out=bufB[:, :], in_=d_lbg.ap()[1:2, :].broadcast_to([128, 512]))], W=['bufB'])
        tt('dve', bufC[:, :], bufA[:, :], bufB[:, :], ALU.subtract, ['bufA', 'bufB'], ['bufC'])
        act(bufA[:, :], bufC[:, :], AF.Tanh, ['bufC'], ['bufA'], scale=0.5)
        ts('dve', C0[:, :], bufA[:, :], 0.5, 0.5, ALU.mult, ALU.add, ['bufA'], ['C0'])
        ts('dve', C1[:, :], bufA[:, :], -0.5, 0.5, ALU.mult, ALU.add, ['bufA'], ['C1'])
        cp('dve', ident[:, :], cstf[:, 0:128], ['cstf'], ['ident'])
        cp('dve', Dm0[:, :], cstf[:, 256:384], ['cstf'], ['Dm0'])
        tzA = kTr[:, 0, :, :].rearrange("p a b -> p (a b)").bitcast(F32)
        tzB = kTr[:, 1, :, :].rearrange("p a b -> p (a b)").bitcast(F32)
        dma('pool', lambda e: [e.dma_start(out=tzA, in_=d_tz3.ap())], W=[('kT', 0)])
        dma('pool', lambda e: [e.dma_start(out=tzB, in_=d_tz4.ap())], W=[('kT', 1)])

        def build_bias_tiles():
            for h in range(8):
                ts('dve', D3[:, h, :], tzA[:, h * 128:(h + 1) * 128], cfar[:, h:h + 1], 8.0, ALU.subtract, ALU.mult,
                   [('kT', 0), 'cfar'], [('D3', h)])
                ts('dve', tzB[:, h * 128:(h + 1) * 128], tzB[:, h * 128:(h + 1) * 128], cfar[:, h:h + 1], 8.0,
                   ALU.subtract, ALU.mult, [('kT', 1), 'cfar'], [('kT', 1)])
                tt('dve', D4[:, h, :], tzB[:, h * 128:(h + 1) * 128], cstf[:, 384:512], ALU.add,
                   [('kT', 1), 'cstf'], [('D4', h)])
            op('pool', lambda e: [e.memset(qTz[:, :, :], 0.0)], W=['qTz'])
            op('pool', lambda e: [e.memset(vr[:, :, :, :, 64:65], 1.0)], W=[('vr', 0), ('vr', 1)])

        op('pool', lambda e: [e.memset(Sa[:, :, :], 0.0)], W=['Sa'])

        wslots = [xres[:, 0, 0:512], xres[:, 0, 512:1024], xres[:, 1, 0:512], xres[:, 1, 512:1024]]
        wkeys = [('xres', 0, 0), ('xres', 0, 1), ('xres', 1, 0), ('xres', 1, 1)]
        cast_engs = ['pool', 'dve', 'pool', 'act']
        wi = [0]

        def load_w(src_ap, dst_ap, scale_ap, wkey):
            i = wi[0]; wi[0] += 1
            slot = wslots[i % 4]; sk = wkeys[i % 4]
            dma('sp', lambda e: [e.dma_start(out=slot, in_=src_ap)], W=[sk])
            eng = cast_engs[i % 4]
            if eng == 'act':
                op('act', lambda e: [e.activation(out=dst_ap, in_=slot, func=AF.Copy, scale=scale_ap)], [sk, 'gcol'], [wkey])
            elif eng == 'pool':
                ts(eng, dst_ap, slot, scale_ap, 1.0, ALU.mult, ALU.mult, [sk, 'gcol'], [wkey])
            else:
                ts(eng, dst_ap, slot, scale_ap, None, ALU.mult, None, [sk, 'gcol'], [wkey])

        wtasks = []

        def load_win_cols(g):
            for kc in range(8):
                wtasks.append((d_win.ap()[kc * 128:(kc + 1) * 128, g * 512:(g + 1) * 512],
                               win[:, kc, g * 512:(g + 1) * 512], gcol[:, kc:kc + 1], ('win', g)))

        def load_wout():
            for kc in range(8):
                for n in range(2):
                    wtasks.append((d_wout.ap()[kc * 128:(kc + 1) * 128, n * 512:(n + 1) * 512],
                                   wout[:, kc, n * 512:(n + 1) * 512], 0.5, ('wout', n)))

        def pump(n):
            for _ in range(n):
                if wtasks:
                    load_w(*wtasks.pop(0))

        for g in (5, 6, 4, 7, 1, 2, 3, 0):
            load_win_cols(g)
        load_wout()

        nctr = [0]

        def stage_N1(xa, t, act_sq=False):
            s = nctr[0] % 2; nctr[0] += 1
            dma('sp', lambda e: [e.dma_start(out=xin[:, s, :], in_=xa)], W=[('xin', s)])
            if act_sq:
                op('act', lambda e: [e.activation(out=xsb[:, s, :], in_=xin[:, s, :], func=AF.Square, accum_out=sm(t, t + 1))],
                   [('xin', s)], [('xsb', s), ('ss', t)])
            else:
                stt(xsb[:, s, :], xin[:, s, :], 1.0, xin[:, s, :], ALU.mult, ALU.mult,
                    [('xin', s)], [('xsb', s), ('ss', t)], accum=sm(t, t + 1))
            rstd_from(sm(t, t + 1), sm(4 + t, 5 + t), sm(8 + t, 9 + t), 1024.0, [('ss', t)], [('rstd', t)], ('lnv', t))
            op('act', lambda e: [e.activation(out=xsb[:, s, :], in_=xin[:, s, :], func=AF.Copy, scale=sm(8 + t, 9 + t))],
               [('xin', s), ('rstd', t)], [('xsb', s)])
            return s

        def stage_N2(s, t):
            b = S.bank()
            op('pe', lambda e: [e.transpose(out=psb(b)[:, kc * 128:(kc + 1) * 128],
                                            in_=xsb[:, s, kc * 128:(kc + 1) * 128], identity=ident[:, :])
                                for kc in range(8)],
               [('xsb', s), 'ident'], [('ps', b)])
            cp('dve', xnT[:, :, t * 128:(t + 1) * 128],
               psb(b).rearrange("p (k t) -> p k t", k=8), [('ps', b)], [('xnT', t)])

        def stage_N_tile(xa, t, act_sq=False):
            stage_N2(stage_N1(xa, t, act_sq), t)

        def stage_N(x_aps):
            for t, xa in enumerate(x_aps):
                stage_N_tile(xa, t)
            return len(x_aps)

        def proj_tm(t, g):
            b = S.bank()
            op('pe', lambda e: [e.matmul(psf(b), lhsT=xnT[:, kc, t * 128:(t + 1) * 128],
                                         rhs=win[:, kc, g * 512:(g + 1) * 512], start=(kc == 0), stop=(kc == 7))
                                for kc in range(8)],
               [('xnT', t), ('win', g)], [('ps', b)])
            return b

        def proj_fm(c, nt):
            b = S.bank()
            op('pe', lambda e: [e.matmul(ps[b][:, 0:nt * 128], lhsT=win[:, kc, c * 128:(c + 1) * 128],
                                         rhs=xnT[:, kc, 0:nt * 128], start=(kc == 0), stop=(kc == 7))
                                for kc in range(8)],
               [('xnT', t) for t in range(nt)] + [('win', c // 4)], [('ps', b)])
            return b

        DEF = dict(A=(bufA, 'bufA'), B=(bufB, 'bufB'), C=(bufC, 'bufC'), kt=(kt[:, :], 'kt'), vb=(vb[:, :], 'vb'))

        def hgrn_gates(t, bs=DEF):
            (A, kA), (B, kB), (vbx, kv) = bs['A'], bs['B'], bs['vb']
            b = proj_tm(t, 5)
            act(A[:, :], psf(b), AF.Exp, [('ps', b)], [kA], scale=-1.0)
            act(A[:, :], A[:, :], AF.Ln, [kA], [kA], bias=1.0)
            act(A[:, :], A[:, :], AF.Exp, [kA], [kA], scale=-1.0)
            tt('pool', B[:, :], A[:, :], C1[:, :], ALU.mult, [kA, 'C1'], [kB])
            tt('dve', A[:, :], B[:, :], C0[:, :], ALU.add, [kB, 'C0'], [kA])
            ts('dve', B[:, :], A[:, :], -1.0, 1.0, ALU.mult, ALU.add, [kA], [kB])
            act(A[:, :], A[:, :], AF.Ln, [kA], [kA])
            b = proj_tm(t, 6)
            cp('act', vbx, psf(b), [('ps', b)], [kv])

        def hgrn_cum(t, full, bs=DEF, ebx=None, kex=None):
            (A, kA), (B, kB), (C, kC), (ktx, kk_) = bs['A'], bs['B'], bs['C'], bs['kt']
            b1 = S.bank()
            op('pe', lambda e: [e.matmul(psf(b1), lhsT=tri, rhs=A[:, :], start=True, stop=True)],
               [kA, 'cstf'], [('ps', b1)])
            act(C[:, :], psf(b1), AF.Exp, [('ps', b1)], [kC], scale=-1.0)
            tt('dve', ktx, B[:, :], C[:, :], ALU.mult, [kB, kC], [kk_])
            b2 = S.bank()
            if full:
                op('pe', lambda e: [e.matmul(ps[b2][:, h * 128:(h + 1) * 128], lhsT=A[:, h * 128:(h + 1) * 128],
                                             rhs=tri, start=True, stop=True) for h in range(4)],
                   [kA, 'cstf'], [('ps', b2)])
                act(ebT[:, :, :], psf(b2).rearrange("p (h t) -> p h t", h=4), AF.Exp, [('ps', b2)], ['ebT'])
                return (ebT[:, :, 63:64], ebT[:, :, 127:128])
            else:
                op('pe', lambda e: [e.matmul(ps[b2][:, h * 2:(h + 1) * 2], lhsT=A[:, h * 128:(h + 1) * 128],
                                             rhs=cstf[:, 191:256:64], start=True, stop=True) for h in range(4)],
                   [kA, 'cstf'], [('ps', b2)])
                ebv = ebx.rearrange("p (h t) -> p h t", h=4)
                act(ebv, ps[b2][:, 0:8].rearrange("p (h t) -> p h t", h=4), AF.Exp, [('ps', b2)], [kex])
                return (ebv[:, :, 0:1], ebv[:, :, 1:2])

        def hgrn_U(c, bs=DEF):
            (ktx, kk_), (vbx, kv) = bs['kt'], bs['vb']
            b = S.bank()
            op('pe', lambda e: [e.matmul(ps[b][:, h * 128:(h + 1) * 128], lhsT=ktx[c * 64:(c + 1) * 64, h * 128:(h + 1) * 128],
                                         rhs=vbx[c * 64:(c + 1) * 64, h * 128:(h + 1) * 128], start=True, stop=True)
                                for h in range(4)],
               [kk_, kv], [('ps', b)])
            return b

        def bc(ap):
            return ap.broadcast_to([128, 4, 128])

        def chain_step(Sin, kin, Sout, kout, bU, eb, keb='ebT'):
            tt('dve', tmpS[:, :, :], psf(bU).rearrange("p (h v) -> p h v", h=4), Sin[:, :, :], ALU.add,
               [('ps', bU), kin], ['tmpS'])
            tt('dve', Sout[:, :, :], tmpS[:, :, :], bc(eb), ALU.mult, ['tmpS', keb], [kout])

        PRE_T = 48 if STOP >= 3 else (8 if STOP == 2 else 0)
        osb_bf = osb[:, :].bitcast(BF16)
        osq_bf = osq[:, :].bitcast(BF16)
        setsA = [(bufA[:, :], 'bufA'), (tta[:, :], 'tta'), (yres[:, 0:512], ('yres', 0)), (yres[:, 512:1024], ('yres', 1))]
        setsB = [(bufB[:, :], 'bufB'), (g2f[:, :], 'g2f'), (kvst[:, 0, :], ('kvst', 0))]
        setsC = [(bufC[:, :], 'bufC'), (tfm[:, :], 'tfm')]
        setsK = [(kt[:, :], 'kt'), (osb_bf[:, 0:512], ('osbh', 0))]
        setsV = [(vb[:, :], 'vb'), (osb_bf[:, 512:1024], ('osbh', 1)), (osq_bf[:, 0:512], ('osqh', 0)),
                 (osq_bf[:, 512:1024], ('osqh', 1)), (ga2[:, :], 'ga2')]

        def pre_x(i):
            return d_xpre.ap()[i * 128:(i + 1) * 128, :]

        def pA1(i):
            t = i % 4
            A, kA = setsA[i % 4]
            V, kV = setsV[i % 5]
            b = proj_tm(t, 5)
            act(A, psf(b), AF.Exp, [('ps', b)], [kA], scale=-1.0)
            act(A, A, AF.Ln, [kA], [kA], bias=1.0)
            act(A, A, AF.Exp, [kA], [kA], scale=-1.0)
            b = proj_tm(t, 6)
            cp('dve', V, psf(b), [('ps', b)], [kV])

        def pA2(i):
            A, kA = setsA[i % 4]
            B, kB = setsB[i % 3]
            tt('pool', B, A, C1[:, :], ALU.mult, [kA, 'C1'], [kB])
            tt('pool', A, B, C0[:, :], ALU.add, [kB, 'C0'], [kA])
            ts('dve', B, A, -1.0, 1.0, ALU.mult, ALU.add, [kA], [kB])
            act(A, A, AF.Ln, [kA], [kA])

        def pB1(i):
            A, kA = setsA[i % 4]
            C, kC = setsC[i % 2]
            b1 = S.bank()
            op('pe', lambda e: [e.matmul(psf(b1), lhsT=tri, rhs=A, start=True, stop=True)], [kA, 'cstf'], [('ps', b1)])
            act(C, psf(b1), AF.Exp, [('ps', b1)], [kC], scale=-1.0)
            b2 = S.bank()
            op('pe', lambda e: [e.matmul(ps[b2][:, h * 2:(h + 1) * 2], lhsT=A[:, h * 128:(h + 1) * 128],
                                         rhs=cstf[:, 191:256:64], start=True, stop=True) for h in range(4)],
               [kA, 'cstf'], [('ps', b2)])
            ebv = ebp[:, i % 3, :].rearrange("p (h t) -> p h t", h=4)
            act(ebv, ps[b2][:, 0:8].rearrange("p (h t) -> p h t", h=4), AF.Exp, [('ps', b2)], [('ebp', i % 3)])

        def pB2(i):
            B, kB = setsB[i % 3]
            C, kC = setsC[i % 2]
            K, kK = setsK[i % 2]
            tt('pool', K, B, C, ALU.mult, [kB, kC], [kK])

        def pC(i):
            bs = dict(kt=setsK[i % 2], vb=setsV[i % 5])
            ebv = ebp[:, i % 3, :].rearrange("p (h t) -> p h t", h=4)
            bU0 = hgrn_U(0, bs)
            chain_step(Sa, 'Sa', Sb_, 'Sb', bU0, ebv[:, :, 0:1], ('ebp', i % 3))
            bU1 = hgrn_U(1, bs)
            chain_step(Sb_, 'Sb', Sa, 'Sa', bU1, ebv[:, :, 1:2], ('ebp', i % 3))

        if PRE_T >= 4:
            s0_ = stage_N1(pre_x(0), 0, act_sq=True)
            s1_ = stage_N1(pre_x(1), 1, act_sq=True)
            pump(8)
            stage_N2(s0_, 0)
            s2_ = stage_N1(pre_x(2), 2, act_sq=True)
            stage_N2(s1_, 1)
            s3_ = stage_N1(pre_x(3), 3, act_sq=True)
            pump(8)
            stage_N2(s2_, 2)
            stage_N2(s3_, 3)
        else:
            pump(16)
        nslot = {}
        spref = {}
        if PRE_T > 4:
            nslot[4] = stage_N1(pre_x(4), 0, act_sq=True)
        for i in range(PRE_T + 4 if PRE_T else 0):
            if i < PRE_T:
                pA1(i)
                if i + 4 < PRE_T:
                    stage_N2(nslot.pop(i + 4), i % 4)
                if i + 5 < PRE_T:
                    nslot[i + 5] = stage_N1(pre_x(i + 5), (i + 1) % 4, act_sq=True)
                pump(2)
            if 0 <= i - 1 < PRE_T:
                pA2(i - 1)
            if 0 <= i - 2 < PRE_T:
                pB1(i - 2)
            if 0 <= i - 3 < PRE_T:
                pB2(i - 3)
            if 0 <= i - 4 < PRE_T:
                pC(i - 4)
            if PRE_T and STOP >= 4 and i in (PRE_T - 4, PRE_T - 3):
                ts_ = i - (PRE_T - 4)
                spref[ts_] = stage_N1(d_xs.ap()[ts_ * 128:(ts_ + 1) * 128, :], ts_, act_sq=True)
        pump(1000)
        op('pool', lambda e: [e.memset(osb[:, 0:2], 0.0)], W=[('osbh', 0), ('osbh', 1), 'osb'])
        op('pool', lambda e: [e.memset(osq[:, 0:2], 0.0)], W=[('osqh', 0), ('osqh', 1), 'osq'])
        cp('pool', Lsave[:, :], Sa[:, :, :].rearrange("p h v -> p (h v)"), ['Sa'], ['Lsave'])

        def fold_start_state():
            cp('pool', Sa[:, :, :].rearrange("p h v -> p (h v)"), Lsave[:, :], ['Lsave'], ['Sa'])

        def fm_evacs(nt, slot, do_q=True):
            ntok = nt * 128
            for hp in range(4):
                b = proj_fm(4 + hp, nt)
                cp('dve' if hp % 2 else 'act', kTr[:, slot, hp, 0:ntok], ps[b][:, 0:ntok], [('ps', b)], [('kT', slot)])
                yield
            if not do_q:
                return
            for hp in range(4):
                b = proj_fm(hp, nt)
                cp('act', qTz[0:64, 2 * hp, 0:ntok], ps[b][0:64, 0:ntok], [('ps', b)], ['qTz'])
                cp('dve', qTz[64:128, 2 * hp + 1, 0:ntok], ps[b][64:128, 0:ntok], [('ps', b)], ['qTz'])
                yield
            for h in range(4):
                b = proj_fm(16 + h, nt)
                act(tfm[:, 0:ntok], ps[b][:, 0:ntok], AF.Tanh, [('ps', b)], ['tfm'], scale=0.5)
                stt(hqs[:, h, 0:ntok], tfm[:, 0:ntok], 1.0, ps[b][:, 0:ntok], ALU.add, ALU.mult, ['tfm', ('ps', b)], ['hqs'])
                yield

        def av_evac(t, slot, out_ap=None, vslot=0):
            b = proj_tm(t, 2)
            cp('act', vr[:, slot, t, :, 0:64], psf(b).rearrange("p (h d) -> p h d", h=8), [('ps', b)], [('vr', slot)])
            if out_ap is not None:
                cp('act', kvst[:, vslot, :], psf(b), [('ps', b)], [('kvst', vslot)])
                dma('pool', lambda e: [e.dma_start(out=out_ap, in_=kvst[:, vslot, :])], R=[('kvst', vslot)], W=['okv'])

        def ak_tm_out(t, out_ap, vslot=1):
            b = proj_tm(t, 1)
            cp('dve', kvst[:, vslot, :], psf(b), [('ps', b)], [('kvst', vslot)])
            dma('pool', lambda e: [e.dma_start(out=out_ap, in_=kvst[:, vslot, :])], R=[('kvst', vslot)], W=['okv2'])

        def gate_a(t):
            b = proj_tm(t, 3)
            act(tta[:, :], psf(b), AF.Tanh, [('ps', b)], ['tta'], scale=0.5)
            stt(ga2[:, :], tta[:, :], 1.0, psf(b), ALU.add, ALU.mult, ['tta', ('ps', b)], ['ga2'])

        def gate_b(t):
            b = proj_tm(t, 7)
            act(ttg[:, :], psf(b), AF.Tanh, [('ps', b)], ['tta'], scale=0.5)
            stt(g2f[:, :], ttg[:, :], 1.0, psf(b), ALU.add, ALU.mult, ['tta', ('ps', b)], ['g2f'])
            tt('pool', gbg[:, :], g2f[:, :], Gn[:, :], ALU.mult, ['g2f', 'Gn'], ['gbg'])

        def attn_evac(bO, half):
            r = sm(28 + 4 * half, 32 + 4 * half)
            pv = psf(bO)[:, 0:260].rearrange("p (h d) -> p h d", h=4)
            op('dve', lambda e: [e.reciprocal(out=r, in_=pv[:, :, 64:65].rearrange("p h o -> p (h o)"))],
               [('ps', bO)], [('rinv', half)])
            for hh in range(4):
                h = half * 4 + hh
                stt(oa[:, h * 64:(h + 1) * 64], pv[:, hh, 0:64], r[:, hh:hh + 1], ga2[:, h * 64:(h + 1) * 64],
                    ALU.mult, ALU.mult, [('ps', bO), ('rinv', half), 'ga2'], ['oa'])

        def attn_prompt(p, slot):
            pslot = 1 - slot
            S.pctr['a'] = 0
            srcs = []
            for j in range(5):
                rel = p - 4 + j
                srcs.append((pslot, 4 + rel) if rel < 0 else (slot, rel))

            def emit_qk(h):
                hp = h // 2
                pi = h % 2
                bX = S.bank('a'); bY = S.bank('a')

                def qk(e):
                    out = []
                    for j in range(5):
                        sl, tl = srcs[j]
                        dst = ps[bX][:, j * 128:(j + 1) * 128] if j < 4 else ps[bY][:, 0:128]
                        bias = {0: Dm0[:, :], 3: D3[:, h, :], 4: D4[:, h, :]}.get(j)
                        out.append(e.matmul(dst, lhsT=kTr[:, sl, hp, tl * 128:(tl + 1) * 128],
                                            rhs=qTz[:, h, p * 128:(p + 1) * 128], start=True, stop=(bias is None)))
                        if bias is not None:
                            out.append(e.matmul(dst, lhsT=ident[:, :], rhs=bias, start=False, stop=True))
                    return out
                op('pe', qk, [('kT', 0), ('kT', 1), 'qTz', 'ident', 'Dm0', ('D3', h), ('D4', h)], [('ps', bX), ('ps', bY)])
                act(pT[:, pi, 0:4, :], psf(bX).rearrange("p (j q) -> p j q", j=4), AF.Exp, [('ps', bX), 'cfar'],
                    [('pT', pi, 0)], bias=cfar[:, h:h + 1], scale=0.125)
                act(pT[:, pi, 4, :], ps[bY][:, 0:128], AF.Exp, [('ps', bY), 'cfar'], [('pT', pi, 1)],
                    bias=cfar[:, h:h + 1], scale=0.125)

            def emit_pv(h):
                pi = h % 2
                bO = 6
                hh = h % 4

                def pvf(e):
                    out = []
                    for j in range(5):
                        sl, tl = srcs[j]
                        out.append(e.matmul(ps[bO][:, hh * 65:(hh + 1) * 65], lhsT=pT[:, pi, j, :],
                                            rhs=vr[:, sl, tl, h, :], start=(j == 0), stop=(j == 4)))
                    return out
                op('pe', pvf, [('pT', pi, 0), ('pT', pi, 1), ('vr', 0), ('vr', 1)], [('ps', bO)])
                if hh == 3:
                    attn_evac(bO, h // 4)

            for h in range(9):
                if h < 8:
                    emit_qk(h)
                if h > 0:
                    emit_pv(h - 1)
                yield

        def hgrn_main(t, sample_states=None):
            e0, e1 = hgrn_cum(t, full=True)
            yield
            stt(qtT[:, :, :], hqs[:, :, t * 128:(t + 1) * 128], 0.5, ebT[:, :, :], ALU.mult, ALU.mult, ['hqs', 'ebT'], ['qtT'])
            b = S.bank()
            op('pe', lambda e: [e.transpose(out=psb(b)[:, h * 128:(h + 1) * 128], in_=kt[:, h * 128:(h + 1) * 128],
                                            identity=ident[:, :]) for h in range(4)], ['kt', 'ident'], [('ps', b)])
            cp('act', ktT[:, :, :], psb(b)[:, 0:512].rearrange("p (h t) -> p h t", h=4), [('ps', b)], ['ktT'])
            yield
            if sample_states is None:
                cp('pool', Sbf0[:, :, :], Sa[:, :, :], ['Sa'], ['Sbf0'])
                bU0 = hgrn_U(0)
                chain_step(Sa, 'Sa', Sb_, 'Sb', bU0, e0)
                yield
                cp('act', Sbf1[:, :, :], Sb_[:, :, :], ['Sb'], ['Sbf1'])
                bU1 = hgrn_U(1)
                chain_step(Sb_, 'Sb', Sa, 'Sa', bU1, e1)
                yield
            else:
                q0, q1 = sample_states
                dma('sp', lambda e: [e.dma_start(out=Sa[:, :, :], in_=d_st.ap()[q0].rearrange("h k v -> k h v")),
                                     e.dma_start(out=Sb_[:, :, :], in_=d_st.ap()[q1].rearrange("h k v -> k h v"))],
                    W=['Sa', 'Sb'], n=2)
                cp('pool', Sbf0[:, :, :], Sa[:, :, :], ['Sa'], ['Sbf0'])
                cp('pool', Sbf1[:, :, :], Sb_[:, :, :], ['Sb'], ['Sbf1'])
                for (Sx, kx, c, eb, q) in ((Sa, 'Sa', 0, e0, q0), (Sb_, 'Sb', 1, e1, q1)):
                    bU = hgrn_U(c)
                    tt('dve', tmpS[:, :, :], psf(bU).rearrange("p (h v) -> p h v", h=4), Sx[:, :, :], ALU.add,
                       [('ps', bU), kx], ['tmpS'])
                    tt('dve', osq[:, :].rearrange("p (h v) -> p h v", h=4), tmpS[:, :, :], bc(eb), ALU.mult,
                       ['tmpS', 'ebT'], ['osq'])
                    dma('pool', lambda e, q=q: [e.dma_start(out=o_ss.ap()[q].rearrange("h k v -> k h v"),
                                                            in_=osq[:, :].rearrange("p (h v) -> p h v", h=4))],
                        R=['osq'], W=['o_ss'])
                    yield
            bA = S.bank()
            op('pe', lambda e: [e.matmul(ps[bA][:, h * 128:(h + 1) * 128], lhsT=ktT[:, h, :], rhs=qtT[:, h, :],
                                         start=True, stop=True) for h in range(4)], ['ktT', 'qtT'], [('ps', bA)])
            tt('dve', ATm[:, :, :], psf(bA).rearrange("p (h t) -> p h t", h=4),
               tri.rearrange("p (o t) -> p o t", o=1).broadcast_to([128, 4, 128]), ALU.mult, [('ps', bA), 'cstf'], ['ATm'])
            yield
            bO = S.bank()

            def of(e):
                out = []
                for h in range(4):
                    cs = slice(h * 128, (h + 1) * 128)
                    out.append(e.matmul(ps[bO][0:64, cs], lhsT=qtT[:, h, 0:64], rhs=Sbf0[:, h, :], start=True, stop=False))
                    out.append(e.matmul(ps[bO][64:128, cs], lhsT=qtT[:, h, 64:128], rhs=Sbf1[:, h, :], start=True, stop=False))
                    out.append(e.matmul(ps[bO][:, cs], lhsT=ATm[:, h, :], rhs=vb[:, cs], start=False, stop=True))
                return out
            op('pe', of, ['qtT', 'Sbf0', 'Sbf1', 'ATm', 'vb'], [('ps', bO)])
            cp('act', osb[:, :], psf(bO), [('ps', bO)], ['osb'])
            tt('dve', osq[:, :], psf(bO), osb[:, :], ALU.mult, [('ps', bO), 'osb'], ['osq'])
            yield
            op('dve', lambda e: [e.tensor_reduce(out=sm(12, 16), in_=osq[:, :].rearrange("p (h v) -> p h v", h=4),
                                                 axis=AX.X, op=ALU.add)], ['osq'], ['ms_o'])
            rstd_from(sm(12, 16), sm(16, 20), sm(20, 24), 128.0, ['ms_o'], ['rstd_o'], 'ln_o')
            yield
            for h in range(4):
                cs = slice(h * 128, (h + 1) * 128)
                stt(ob[:, cs], osb[:, cs], sm(20 + h, 21 + h), gbg[:, cs], ALU.mult, ALU.mult, ['osb', 'rstd_o', 'gbg'], ['ob'])
            yield

        def wide(gen):
            first = True
            while True:
                narrow = S.pools['g']
                nctr_ = S.pctr.get('g', 0)
                S.pools['g'] = [0, 1, 2, 3, 4, 5]
                S.pctr['g'] = 0 if first else S.pctr.get('gw', 0)
                first = False
                try:
                    next(gen)
                except StopIteration:
                    S.pools['g'] = narrow
                    S.pctr['g'] = 0
                    return
                S.pctr['gw'] = S.pctr['g']
                S.pools['g'] = narrow
                S.pctr['g'] = nctr_
                yield

        def drive(*gens):
            gens = [g for g in gens if g is not None]
            while gens:
                for g in list(gens):
                    try:
                        next(g)
                    except StopIteration:
                        gens.remove(g)

        octr = [0]

        def out_stage(x_ap, y_ap):
            b = S.bank()
            op('pe', lambda e: [e.transpose(out=psb(b)[:, c * 128:(c + 1) * 128],
                                            in_=(oa if c < 4 else ob)[:, (c % 4) * 128:(c % 4 + 1) * 128],
                                            identity=ident[:, :]) for c in range(8)], ['oa', 'ob', 'ident'], [('ps', b)])
            cp('act', oT[:, :, :], psb(b).rearrange("p (k t) -> p k t", k=8), [('ps', b)], ['oT'])
            yield
            s = octr[0] % 2; octr[0] += 1
            ys_ = 0
            dma('sp', lambda e: [e.dma_start(out=xres[:, s, :], in_=x_ap)],
                W=[('xres', s, 0), ('xres', s, 1)])
            for n in range(2):
                bY = S.bank()
                op('pe', lambda e, n=n, bY=bY: [e.matmul(psf(bY), lhsT=oT[:, kc, :], rhs=wout[:, kc, n * 512:(n + 1) * 512],
                                                         start=(kc == 0), stop=(kc == 7)) for kc in range(8)],
                   ['oT', ('wout', n)], [('ps', bY)])
                tt('dve', yres[:, n * 512:(n + 1) * 512], psf(bY), xres[:, s, n * 512:(n + 1) * 512], ALU.add,
                   [('ps', bY), ('xres', s, n)], [('yres', n)])
                yield
            stt(yout[:, ys_, :], yres[:, :], 1.0, yres[:, :], ALU.mult, ALU.mult, [('yres', 0), ('yres', 1)],
                [('yout', ys_), 'ssy'], accum=sm(24, 25))
            yield
            rstd_from(sm(24, 25), sm(25, 26), sm(26, 27), 1024.0, ['ssy'], ['ry'], 'lny')
            yield
            stt(yout[:, ys_, :], yres[:, :], sm(26, 27), Gf[:, :], ALU.mult, ALU.mult, [('yres', 0), ('yres', 1), 'ry', 'Gf'],
                [('yout', ys_)])
            dma(YQ, lambda e: [e.dma_start(out=y_ap, in_=yout[:, ys_, :])], R=[('yout', ys_)], W=['oy'])

        def attn_sample(t):
            slot, pslot = 0, 1
            for e_ in range(2):
                q = 2 * t + e_
                for j in range(4):
                    s = nctr[0] % 2; nctr[0] += 1
                    dma('sp', lambda e, q=q, j=j, s=s: [e.dma_start(out=xin[:, s, 0:512], in_=d_ck.ap()[q, j * 128:(j + 1) * 128, :]),
                                                        e.dma_start(out=xin[:, s, 512:1024], in_=d_cv.ap()[q, j * 128:(j + 1) * 128, :])],
                        W=[('xin', s)], n=2)
                    cp('pool', xsb[:, s, 0:512], xin[:, s, 0:512], [('xin', s)], [('xsb', s)])
                    cp('act', vr[:, pslot, j, :, 0:64], xin[:, s, 512:1024].rearrange("p (h d) -> p h d", h=8),
                       [('xin', s)], [('vr', pslot)])
                    b = S.bank()
                    op('pe', lambda e, s=s, b=b: [e.transpose(out=psb(b)[:, hp * 128:(hp + 1) * 128],
                                                              in_=xsb[:, s, hp * 128:(hp + 1) * 128], identity=ident[:, :])
                                                  for hp in range(4)], [('xsb', s), 'ident'], [('ps', b)])
                    cp('dve', kTr[:, pslot, :, j * 128:(j + 1) * 128], psb(b)[:, 0:512].rearrange("p (h t) -> p h t", h=4),
                       [('ps', b)], [('kT', pslot)])
                    yield
                r0 = e_ * 64
                qc = slice(t * 128 + r0, t * 128 + r0 + 64)

                def emit_qk(h, r0=r0, qc=qc):
                    hp = h // 2
                    pi = h % 2
                    bX = S.bank('a'); bY = S.bank('a')

                    def qk(e):
                        out = []
                        for j in range(4):
                            dst = ps[bX][:, j * 128:j * 128 + 64]
                            bias = {0: Dm0[:, 0:64], 3: D3[:, h, 0:64]}.get(j)
                            out.append(e.matmul(dst, lhsT=kTr[:, pslot, hp, j * 128:(j + 1) * 128], rhs=qTz[:, h, qc],
                                                start=True, stop=(bias is None)))
                            if bias is not None:
                                out.append(e.matmul(dst, lhsT=ident[:, :], rhs=bias, start=False, stop=True))
                        dst = ps[bY][r0:r0 + 64, 0:64]
                        out.append(e.matmul(dst, lhsT=kTr[:, slot, hp, qc], rhs=qTz[:, h, qc], start=True, stop=False))
                        out.append(e.matmul(dst, lhsT=ident[0:64, 0:64], rhs=D4[0:64, h, 0:64], start=False, stop=True))
                        return out
                    op('pe', qk, [('kT', 0), ('kT', 1), 'qTz', 'ident', 'Dm0', ('D3', h), ('D4', h)], [('ps', bX), ('ps', bY)])
                    act(pT[:, pi, 0:4, 0:64], psf(bX).rearrange("p (j q) -> p j q", j=4)[:, :, 0:64], AF.Exp,
                        [('ps', bX), 'cfar'], [('pT', pi, 0)], bias=cfar[:, h:h + 1], scale=0.125)
                    act(pT[r0:r0 + 64, pi, 4, 0:64], ps[bY][r0:r0 + 64, 0:64], AF.Exp, [('ps', bY), 'cfar'],
                        [('pT', pi, 1)], bias=cfar[r0:r0 + 64, h:h + 1], scale=0.125)

                def emit_pv(h, r0=r0):
                    pi = h % 2
                    bO = 6 + h // 4
                    hh = h % 4

                    def pvf(e):
                        out = []
                        dst = ps[bO][r0:r0 + 64, hh * 65:(hh + 1) * 65]
                        for j in range(4):
                            out.append(e.matmul(dst, lhsT=pT[:, pi, j, 0:64], rhs=vr[:, pslot, j, h, :],
                                                start=(j == 0), stop=False))
                        out.append(e.matmul(dst, lhsT=pT[r0:r0 + 64, pi, 4, 0:64], rhs=vr[r0:r0 + 64, slot, t, h, :],
                                            start=False, stop=True))
                        return out
                    op('pe', pvf, [('pT', pi, 0), ('pT', pi, 1), ('vr', 0), ('vr', 1)], [('ps', bO)])

                for h in range(9):
                    if h < 8:
                        emit_qk(h)
                    if h > 0:
                        emit_pv(h - 1)
                    yield
            attn_evac(6, 0)
            attn_evac(7, 1)
            yield

        def sample_block():
            slot = 0
            for t in range(2):
                if t in spref:
                    stage_N2(spref.pop(t), t)
                else:
                    stage_N_tile(d_xs.ap()[t * 128:(t + 1) * 128, :], t)
            drive(fm_evacs(2, slot))
            for t in range(2):
                av_evac(t, slot, o_vs.ap()[t * 128:(t + 1) * 128, :], vslot=0)
                ak_tm_out(t, o_ks.ap()[t * 128:(t + 1) * 128, :], vslot=1)
            pend = None
            for t in range(2):
                hgrn_gates(t)
                gate_a(t)
                gate_b(t)
                drive(hgrn_main(t, sample_states=(2 * t, 2 * t + 1)), attn_sample(t), pend)
                pend = out_stage(d_xs.ap()[t * 128:(t + 1) * 128, :], o_ys.ap()[t * 128:(t + 1) * 128, :])
            return pend

        build_bias_tiles()
        S.pools['g'] = [4, 5]
        pend_s = sample_block() if STOP >= 4 else None

        NPB_main = 0 if STOP < 5 else (STOP - 4 if STOP < 8 else NPB)

        def halo_gen():
            hp_ = {}
            for t in range(4):
                if t not in hp_:
                    hp_[t] = stage_N1(d_xh.ap()[t * 128:(t + 1) * 128, :], t)
                if t + 1 < 4:
                    hp_[t + 1] = stage_N1(d_xh.ap()[(t + 1) * 128:(t + 2) * 128, :], t + 1)
                stage_N2(hp_.pop(t), t)
                yield
            yield from fm_evacs(4, 0, do_q=False)
            for t in range(4):
                av_evac(t, 0)
                yield
            cp('pool', vr[:, 0, :, :, 64:65].rearrange("p t h o -> p (t h o)"), pcs[:, 0:1].broadcast_to([128, 32]),
               ['pcs', ('vr', 0)], [('vr', 0)])

        drive(wide(halo_gen()), wide(pend_s) if pend_s is not None else None)

        fold_start_state()

        S.pools['g'] = [5, 7, 4]
        S.pctr['g'] = 0
        pending = [None]

        pref = {}

        def xtile(n, t):
            return d_xp.ap()[(n * 4 + t) * 128 + NOFF:(n * 4 + t + 1) * 128 + NOFF, :]

        def prefetch_gen(n):
            for t in range(2):
                pref[(n, t)] = stage_N1(xtile(n, t), t)
                yield

        def head_gen(n, slot):
            for t in range(4):
                if (n, t) not in pref:
                    pref[(n, t)] = stage_N1(xtile(n, t), t)
                if t + 1 < 4 and (n, t + 1) not in pref:
                    pref[(n, t + 1)] = stage_N1(xtile(n, t + 1), t + 1)
                stage_N2(pref.pop((n, t)), t)
                yield
            yield from fm_evacs(4, slot)

        def inproj_gen(t, slot, last):
            hgrn_gates(t)
            yield
            av_evac(t, slot, o_vp.ap()[t * 128:(t + 1) * 128, :] if last else None, vslot=0)
            yield
            if last:
                ak_tm_out(t, o_kp.ap()[t * 128:(t + 1) * 128, :], vslot=1)
                yield
            gate_a(t)
            yield
            gate_b(t)
            yield

        for n in (MAIN_BLOCKS if MAIN_BLOCKS is not None else range(NPB_main)):
            slot = (n + 1) % 2
            if n == 1:
                op('pool', lambda e: [e.memset(vr[:, 0, :, :, 64:65], 1.0)], R=[('vr', 0)], W=[('vr', 0)])
            drive(wide(head_gen(n, slot)), wide(pending[0]) if pending[0] is not None else None)
            pending[0] = None
            last = (n == NPB - 1)
            for t in range(4):
                row = (n * 4 + t) * 128
                drive(wide(inproj_gen(t, slot, last)))
                blocks_ = list(MAIN_BLOCKS if MAIN_BLOCKS is not None else range(NPB_main))
                nxt = blocks_[blocks_.index(n) + 1] if (t == 3 and blocks_.index(n) + 1 < len(blocks_)) else None
                drive(hgrn_main(t), pending[0], attn_prompt(t, slot), prefetch_gen(nxt) if nxt is not None else None)
                pending[0] = None
                pending[0] = out_stage(d_xp.ap()[row + XOFF:row + XOFF + 128, :],
                                       o_ypl[(row + YOFF) // 512].ap()[(row + YOFF) % 512:(row + YOFF) % 512 + 128, :])
        drive(pending[0])

        dma('pool', lambda e: [e.dma_start(out=o_sp.ap().rearrange("h k v -> k h v"), in_=Sa[:, :, :])],
            R=['Sa'], W=['o_sp'])

        with nc.Block() as block:
            S.emit(block)
    return nc


_NC_CACHE = {}


def _consts():
    cst = np.zeros((128, 512), np.float32)
    cst[:, 0:128] = np.eye(128, dtype=np.float32)
    s = np.arange(128)[:, None]; t = np.arange(128)[None, :]
    cst[:, 128:256] = ((s // 64 == t // 64) & (s <= t)).astype(np.float32)
    NEG = -30000.0
    k = s; q = t
    cst[:, 256:384] = np.where((q >= 64) & (k < 64), NEG, 0.0)
    cst[:, 384:512] = np.where((k >= 64) & (q < 64), NEG, 0.0)
    return cst


def kernel(x_prompt, x_sample, cache_attn_k, cache_attn_v, state_hgrn, ln_in_g, w_in,
           rel_bias, lb_gamma, hg_norm_g, w_out, ln_f_g):
    f = lambda a: np.ascontiguousarray(np.asarray(a, dtype=np.float32))
    x_prompt, x_sample = f(x_prompt), f(x_sample)
    ck, cv, st = f(cache_attn_k), f(cache_attn_v), f(state_hgrn)
    w_in_, w_out_ = f(w_in)[0], f(w_out)[0]
    rb = f(rel_bias)[0]
    kk_, qq_ = np.arange(128)[:, None], np.arange(128)[None, :]
    idx3 = np.clip(128 + qq_ - kk_, -128, 128) + 128
    idx4 = np.clip(qq_ - kk_, -128, 128) + 128
    tz3 = np.transpose(rb[:, idx3], (1, 0, 2)).reshape(128, 1024)
    tz4 = np.transpose(rb[:, idx4], (1, 0, 2)).reshape(128, 1024)
    common = {
        "w_in": w_in_, "w_out": w_out_,
        "gin": f(f(ln_in_g)[0].reshape(8, 128).T), "gf": f(ln_f_g).reshape(1, 1024),
        "gn": f(hg_norm_g).reshape(1, 512), "lbg": f(lb_gamma), "tz3": f(tz3), "tz4": f(tz4),
        "rbfar": f(rb[:, 256]).reshape(1, 8), "cst": _consts(),
    }
    in_maps = []
    for c in range(8):
        b, j = c // 4, c % 4
        pc = np.zeros((128, 8), np.float32)
        pc[:, 0] = 1.0 if j > 0 else 0.0
        for jp in range(4):
            pc[:, 1 + jp] = 1.0 if jp < j else 0.0
        m = dict(common)
        m["xp"] = x_prompt[b, 2048 * j:2048 * (j + 1)]
        m["xh"] = x_prompt[b, 2048 * j - 512:2048 * j] if j > 0 else np.zeros((512, 1024), np.float32)
        m["xs"] = x_sample[4 * c:4 * c + 4].reshape(256, 1024)
        xpre = np.zeros((6144, 1024), np.float32)
        if j > 0:
            xpre[6144 - 2048 * j:] = x_prompt[b, 0:2048 * j]
        m["xpre"] = xpre
        m["ck"] = ck[0, 4 * c:4 * c + 4].reshape(4, 512, 512)
        m["cv"] = cv[0, 4 * c:4 * c + 4].reshape(4, 512, 512)
        m["st"] = st[0, 4 * c:4 * c + 4]
        m["pc"] = pc
        in_maps.append({k: np.ascontiguousarray(v) for k, v in m.items()})
    if "nc" not in _NC_CACHE:
        _NC_CACHE["nc"] = build_program()
    nc = _NC_CACHE["nc"]
    res = run_bass_kernel_spmd(nc, in_maps, core_ids=list(range(8))).results
    y_prompt = np.zeros((2, 8192, 1024), np.float32)
    y_sample = np.zeros((32, 64, 1024), np.float32)
    nkp = np.zeros((1, 2, 512, 8, 64), np.float32); nvp = np.zeros((1, 2, 512, 8, 64), np.float32)
    nhp = np.zeros((1, 2, 4, 128, 128), np.float32)
    nks = np.zeros((1, 32, 64, 8, 64), np.float32); nvs = np.zeros((1, 32, 64, 8, 64), np.float32)
    nhs = np.zeros((1, 32, 4, 128, 128), np.float32)
    for c in range(8):
        b, j = c // 4, c % 4
        r = res[c]
        for i in range(4):
            y_prompt[b, 2048 * j + 512 * i:2048 * j + 512 * (i + 1)] = r[f"yp{i}"]
        y_sample[4 * c:4 * c + 4] = r["ys"].reshape(4, 64, 1024)
        nks[0, 4 * c:4 * c + 4] = r["ks"].reshape(4, 64, 8, 64)
        nvs[0, 4 * c:4 * c + 4] = r["vs"].reshape(4, 64, 8, 64)
        nhs[0, 4 * c:4 * c + 4] = r["sso"]
        if j == 3:
            nkp[0, b] = r["kp"].reshape(512, 8, 64)
            nvp[0, b] = r["vp"].reshape(512, 8, 64)
            nhp[0, b] = r["spo"]
    return (y_prompt, y_sample, nkp, nvp, nhp, nks, nvs, nhs)
```
